# Optimizing a Trainium2 kernel written in Bass

```python
import jax, jax.numpy as jnp
from jax import lax
import numpy as np

D_MODEL = 1024
BATCH = 8
SEQ = 8192
DEPTH = 1

HEAD_DIM = 64
ATTN_WIDTH = D_MODEL // 2
N_Q_HEADS = ATTN_WIDTH // HEAD_DIM
N_KV_HEADS = max(1, N_Q_HEADS // 4)
Q_PER_KV = N_Q_HEADS // N_KV_HEADS
KV_WIDTH = N_KV_HEADS * HEAD_DIM
WINDOW = 128
BLOCK = 128
GMLP_WIDTH = D_MODEL - ATTN_WIDTH
GMLP_GROUPS = GMLP_WIDTH // HEAD_DIM
GMLP_GROUP_DIM = GMLP_WIDTH // GMLP_GROUPS
CHUNK = 128
MIX_WIDTH = ATTN_WIDTH + GMLP_WIDTH
D_FF = 4 * D_MODEL
IN_WIDTH = ATTN_WIDTH + 2 * KV_WIDTH + 2 * GMLP_WIDTH
SPLITS = [ATTN_WIDTH, ATTN_WIDTH + KV_WIDTH, ATTN_WIDTH + 2 * KV_WIDTH,
          ATTN_WIDTH + 2 * KV_WIDTH + GMLP_WIDTH]
ALPHA = (2.0 * DEPTH) ** 0.25
BETA = (8.0 * DEPTH) ** -0.25
LN_EPS = 1e-5
NEG_INF = -1e30

kernel_name = "hybrid_swa_sgu_deepnorm_block"


def layer_norm(x, g, b):
    xf = x.astype(jnp.float32)
    mu = jnp.mean(xf, axis=-1, keepdims=True)
    var = jnp.mean(jnp.square(xf - mu), axis=-1, keepdims=True)
    return ((xf - mu) * lax.rsqrt(var + LN_EPS) * g.astype(jnp.float32)
            + b.astype(jnp.float32)).astype(x.dtype)


def rms_norm(x, g):
    xf = x.astype(jnp.float32)
    ms = jnp.mean(jnp.square(xf), axis=-1, keepdims=True)
    return (xf * lax.rsqrt(ms + LN_EPS) * g.astype(jnp.float32)).astype(x.dtype)


def alibi_slopes():
    i = jnp.arange(1, N_Q_HEADS + 1, dtype=jnp.float32)
    return jnp.exp2(-8.0 * i / N_Q_HEADS)


def banded_window_attention(q, k, v, sink):
    B, S = q.shape[0], q.shape[1]
    nb = S // BLOCK
    qb = q.reshape(B, nb, BLOCK, N_KV_HEADS, Q_PER_KV, HEAD_DIM)
    pad = ((0, 0), (BLOCK, BLOCK), (0, 0))
    kp = jnp.pad(k, pad).reshape(B, nb + 2, BLOCK, N_KV_HEADS, HEAD_DIM)
    vp = jnp.pad(v, pad).reshape(B, nb + 2, BLOCK, N_KV_HEADS, HEAD_DIM)
    kb = jnp.concatenate([kp[:, :-2], kp[:, 1:-1], kp[:, 2:]], axis=2)
    vb = jnp.concatenate([vp[:, :-2], vp[:, 1:-1], vp[:, 2:]], axis=2)
    s = jnp.einsum('bnqgrd,bnkgd->bngrqk', qb, kb,
                   preferred_element_type=jnp.float32) * (HEAD_DIM ** -0.5)
    blk = jnp.arange(nb, dtype=jnp.int32)[:, None] * BLOCK
    q_pos = blk + jnp.arange(BLOCK, dtype=jnp.int32)[None, :]
    k_pos = blk - BLOCK + jnp.arange(3 * BLOCK, dtype=jnp.int32)[None, :]
    dist = jnp.abs(q_pos[:, :, None] - k_pos[:, None, :])
    valid = (dist <= WINDOW) & (k_pos[:, None, :] >= 0) & (k_pos[:, None, :] < S)
    slopes = alibi_slopes().reshape(N_KV_HEADS, Q_PER_KV)
    bias = -slopes[None, :, :, None, None] * dist[:, None, None].astype(jnp.float32)
    s = jnp.where(valid[:, None, None], s + bias, NEG_INF)
    sink_b = sink.astype(jnp.float32).reshape(1, 1, N_KV_HEADS, Q_PER_KV, 1, 1)
    m = jnp.maximum(jnp.max(s, axis=-1, keepdims=True), sink_b)
    e = jnp.exp(s - m)
    p = e / (jnp.sum(e, axis=-1, keepdims=True) + jnp.exp(sink_b - m))
    o = jnp.einsum('bngrqk,bnkgd->bnqgrd', p.astype(vb.dtype), vb)
    return o.reshape(B, S, ATTN_WIDTH)


def chunked_spatial_gating(u, vg, ln_g, ln_b, w_s, b_s):
    B, S = u.shape[0], u.shape[1]
    nc = S // CHUNK
    vn = layer_norm(vg, ln_g, ln_b).reshape(B, nc, CHUNK, GMLP_GROUPS, GMLP_GROUP_DIM)
    mixed = jnp.einsum('gts,bnsgc->bntgc', w_s, vn) + b_s.T[None, None, :, :, None]
    return u * mixed.reshape(B, S, GMLP_WIDTH)


def setup_inputs(seed: int = 0) -> dict:
    key = jax.random.key(seed)
    ks = jax.random.split(key, 16)
    f32 = jnp.float32
    L = DEPTH
    x = jax.random.normal(ks[0], (BATCH, SEQ, D_MODEL), f32)
    w_in = jax.random.normal(ks[1], (L, D_MODEL, IN_WIDTH), f32) * D_MODEL ** -0.5
    w_in = w_in.at[:, :, SPLITS[1]:SPLITS[2]].multiply(BETA)
    sink = 0.5 * jax.random.normal(ks[2], (L, N_Q_HEADS), f32)
    gmlp_ln_g = 1.0 + 0.05 * jax.random.normal(ks[3], (L, GMLP_WIDTH), f32)
    gmlp_ln_b = 0.02 * jax.random.normal(ks[4], (L, GMLP_WIDTH), f32)
    w_spatial = jax.random.normal(ks[5], (L, GMLP_GROUPS, CHUNK, CHUNK), f32) * CHUNK ** -0.5
    b_spatial = 1.0 + 0.1 * jax.random.normal(ks[6], (L, GMLP_GROUPS, CHUNK), f32)
    attn_norm_g = 1.0 + 0.05 * jax.random.normal(ks[7], (L, ATTN_WIDTH), f32)
    gmlp_norm_g = 1.0 + 0.05 * jax.random.normal(ks[8], (L, GMLP_WIDTH), f32)
    w_out = jax.random.normal(ks[9], (L, MIX_WIDTH, D_MODEL), f32) * (MIX_WIDTH ** -0.5 * BETA)
    ln1_g = 1.0 + 0.05 * jax.random.normal(ks[10], (L, D_MODEL), f32)
    ln1_b = 0.02 * jax.random.normal(ks[11], (L, D_MODEL), f32)
    w_ff1 = jax.random.normal(ks[12], (L, D_MODEL, D_FF), f32) * (D_MODEL ** -0.5 * BETA)
    w_ff2 = jax.random.normal(ks[13], (L, D_FF, D_MODEL), f32) * (D_FF ** -0.5 * BETA)
    ln2_g = 1.0 + 0.05 * jax.random.normal(ks[14], (L, D_MODEL), f32)
    ln2_b = 0.02 * jax.random.normal(ks[15], (L, D_MODEL), f32)
    return {"x": x, "w_in": w_in, "sink": sink, "gmlp_ln_g": gmlp_ln_g,
            "gmlp_ln_b": gmlp_ln_b, "w_spatial": w_spatial, "b_spatial": b_spatial,
            "attn_norm_g": attn_norm_g, "gmlp_norm_g": gmlp_norm_g, "w_out": w_out,
            "ln1_g": ln1_g, "ln1_b": ln1_b, "w_ff1": w_ff1, "w_ff2": w_ff2,
            "ln2_g": ln2_g, "ln2_b": ln2_b}


def reference(x, w_in, sink, gmlp_ln_g, gmlp_ln_b, w_spatial, b_spatial,
              attn_norm_g, gmlp_norm_g, w_out, ln1_g, ln1_b, w_ff1, w_ff2,
              ln2_g, ln2_b):
    for l in range(DEPTH):
        proj = jnp.einsum('bsd,de->bse', x, w_in[l])
        q, k, v, gu, gv = jnp.split(proj, SPLITS, axis=-1)
        attn = banded_window_attention(q, k, v, sink[l])
        gu = jax.nn.gelu(gu, approximate=False)
        gv = jax.nn.gelu(gv, approximate=False)
        sgu = chunked_spatial_gating(gu, gv, gmlp_ln_g[l], gmlp_ln_b[l],
                                     w_spatial[l], b_spatial[l])
        mixed = jnp.concatenate([rms_norm(attn, attn_norm_g[l]),
                                 rms_norm(sgu, gmlp_norm_g[l])], axis=-1)
        mix_out = jnp.einsum('bse,ed->bsd', mixed, w_out[l])
        x = layer_norm(ALPHA * x + mix_out, ln1_g[l], ln1_b[l])
        h = jnp.square(jax.nn.relu(jnp.einsum('bsd,df->bsf', x, w_ff1[l])))
        ff_out = jnp.einsum('bsf,fd->bsd', h, w_ff2[l])
        x = layer_norm(ALPHA * x + ff_out, ln2_g[l], ln2_b[l])
    return x
```

```python
import numpy as np
from contextlib import ExitStack

import concourse.bass as bass
import concourse.mybir as mybir
from concourse.bass_utils import run_bass_kernel_spmd

F32 = mybir.dt.float32
BF16 = mybir.dt.bfloat16
AF = mybir.ActivationFunctionType
ALU = mybir.AluOpType

D = 1024
DFF = 4096
INW = 1792
NH = 8
ALPHA = 2.0 ** 0.25
EPS = 1e-5
NEG = -30000.0
N_CORES = 8
SEQ = 8192

ENGS = ("pe", "act", "dve", "pool", "sp")


class LT:
    __slots__ = ("name", "w", "rs")

    def __init__(self, name):
        self.name = name
        self.w = None
        self.rs = []


class Buf:
    __slots__ = ("ap", "lt", "extra")

    def __init__(self, ap, name, extra=()):
        self.ap = ap
        self.lt = LT(name)
        self.extra = list(extra)


class Op:
    __slots__ = ("eng", "fn", "alldeps", "deps", "signal", "ticket", "is_dma", "sem", "semval", "idx", "dur",
                 "tbl", "region", "lat", "fin", "ndeps", "users", "boost")


def _d_pe(cols):
    return sum(max(c, 128) * 0.45 + 20 for c in cols)


def _d_dve(n):
    return (n + 151) / 0.96


def _d_act(n):
    return 210 + 0.95 * n


def _d_pool(n):
    return 250 + 2.2 * n


class Rec:
    def __init__(self):
        self.all = []
        self.region = 0
        self.dma_keys = {}
        self.cur_boost = 0.0

    def _lts(self, xs):
        r = []
        for x in xs:
            if isinstance(x, Buf):
                r.append(x.lt)
                r.extend(x.extra)
            else:
                r.append(x)
        return r

    def _new(self, eng, fn, reads, writes, dur, is_dma=False, tbl=None, sem=None, lat=0.0):
        o = Op()
        o.eng = eng
        o.fn = fn
        o.is_dma = is_dma
        o.signal = False
        o.ticket = None
        o.sem = sem
        o.semval = None
        o.dur = dur
        o.tbl = tbl
        o.lat = lat
        o.boost = self.cur_boost
        o.region = self.region
        o.idx = len(self.all)
        deps = set()
        reads = self._lts(reads)
        writes = self._lts(writes)
        for t in reads:
            if t.w is not None:
                deps.add(t.w)
        for t in writes:
            if t.w is not None:
                deps.add(t.w)
            deps.update(t.rs)
        deps.discard(o)
        o.alldeps = deps
        for t in reads:
            t.rs.append(o)
        for t in writes:
            t.w = o
            t.rs = []
        self.all.append(o)
        return o

    def pe(self, fns, reads, writes, cols):
        return self._new("pe", fns, reads, writes, _d_pe(cols))

    def dve(self, fn, reads, writes, n):
        return self._new("dve", fn, reads, writes, _d_dve(n))

    def act(self, fn, reads, writes, n, tbl=None):
        return self._new("act", fn, reads, writes, _d_act(n), tbl=tbl)

    def pool(self, fn, reads, writes, n):
        return self._new("pool", fn, reads, writes, _d_pool(n))

    def dma(self, queue, fn, reads, writes, semkey, nbytes, lat=None):
        assert self.dma_keys.setdefault(semkey, queue) == queue
        issue = 120.0 if queue == "sp" else 1200.0
        if lat is None:
            lat = 2000.0 + nbytes / 150.0
        return self._new(queue, fn, reads, writes, issue, is_dma=True, sem=semkey, lat=lat)

    def barrier(self):
        self.region += 1

    def schedule(self):
        import heapq
        streams = {e: [] for e in ENGS}
        nreg = self.region + 1
        bar = []
        HOP = 100.0
        TBL = 1400.0
        for reg in range(nreg):
            ops = [o for o in self.all if o.region == reg]
            for o in ops:
                o.users = []
                o.ndeps = 0
            for o in ops:
                for d in o.alldeps:
                    if d.region == reg:
                        d.users.append(o)
                        o.ndeps += 1
            tail = {}
            for o in reversed(ops):
                t = 0.0
                for u in o.users:
                    tu = tail[u]
                    if tu > t:
                        t = tu
                tail[o] = t + o.dur + o.lat + HOP + o.boost
            efree = {e: 0.0 for e in ENGS}
            tblstate = [None]
            wait_h = {e: [] for e in ENGS}
            rdy_h = {e: [] for e in ENGS}
            ready_t = {}
            for o in ops:
                if o.ndeps == 0:
                    heapq.heappush(wait_h[o.eng], (0.0, o.idx, o))
                    ready_t[o] = 0.0
            nleft = len(ops)
            local = {e: [] for e in ENGS}
            while nleft:
                best = None
                for e in ENGS:
                    wh, rh = wait_h[e], rdy_h[e]
                    while wh and wh[0][0] <= efree[e]:
                        t, i, o = heapq.heappop(wh)
                        heapq.heappush(rh, (-tail[o], i, o))
                    cand = None
                    if rh:
                        if e == "act":
                            pick = None
                            for ent in heapq.nsmallest(16, rh):
                                o = ent[2]
                                if o.tbl is None or o.tbl == tblstate[0]:
                                    pick = ent
                                    break
                            if pick is None:
                                pick = rh[0]
                                st = efree[e] + TBL
                            else:
                                st = efree[e]
                            cand = (st, pick[0], pick[1], pick[2], pick)
                        else:
                            ent = rh[0]
                            cand = (efree[e], ent[0], ent[1], ent[2], ent)
                    elif wh:
                        t, i, o = wh[0]
                        st = t
                        if e == "act" and o.tbl is not None and o.tbl != tblstate[0]:
                            st = max(t, efree[e] + TBL)
                        cand = (st, -tail[o], i, o, None)
                    if cand is not None and (best is None or cand[:3] < best[:3]):
                        best = cand + (e,)
                st, _, i, o, ent, e = best
                if ent is not None:
                    rdy_h[e].remove(ent)
                    heapq.heapify(rdy_h[e])
                else:
                    heapq.heappop(wait_h[e])
                if e == "act" and o.tbl is not None:
                    tblstate[0] = o.tbl
                efree[e] = st + o.dur
                o.fin = st + o.dur + o.lat + HOP
                local[e].append(o)
                nleft -= 1
                for u in o.users:
                    u.ndeps -= 1
                    ready_t[u] = max(ready_t.get(u, 0.0), o.fin)
                    if u.ndeps == 0:
                        heapq.heappush(wait_h[u.eng], (ready_t[u], u.idx, u))
            for e in ENGS:
                streams[e].extend(local[e])
            bar.append({e: (local[e][-1] if local[e] else None) for e in ENGS})
            self.sim_time = max(efree.values())
            self.sim_times = getattr(self, "sim_times", []) + [self.sim_time]
            self.sim_busy = getattr(self, "sim_busy", []) + [{e: round(sum(o.dur for o in local[e]) / 1e3) for e in ENGS}]
        return streams, bar

    def emit(self, nc, es):
        streams, bar = self.schedule()
        first_of = {}
        for e in ENGS:
            seen = set()
            for o in streams[e]:
                if o.region > 0 and (e, o.region) not in seen:
                    seen.add((e, o.region))
                    first_of[o] = o.region
        for e in ENGS:
            for o in streams[e]:
                deps = set()
                for d in o.alldeps:
                    if d.is_dma:
                        deps.add(d)
                    elif d.eng == "pe" and o.eng == "pe":
                        continue
                    else:
                        d.signal = True
                        deps.add(d)
                o.deps = deps
        for o, reg in first_of.items():
            for r in range(reg):
                for e2, lo in bar[r].items():
                    if lo is not None and not lo.is_dma and not (lo.eng == "pe" and o.eng == "pe"):
                        lo.signal = True
                        o.deps.add(lo)
        engsem = {e: es.enter_context(nc.semaphore("c_" + e)) for e in ENGS if e != "sp"}
        dmasem = {k: es.enter_context(nc.semaphore("d_" + k)) for k in self.dma_keys}
        dcount = {k: 0 for k in self.dma_keys}
        region_dma_tot = {}
        for e in ENGS:
            k = 0
            for o in streams[e]:
                if o.is_dma:
                    dcount[o.sem] += 16
                    o.semval = dcount[o.sem]
                elif o.signal:
                    k += 1
                    o.ticket = k
        tot = {k: 0 for k in self.dma_keys}
        reg_tot = []
        for r in range(self.region + 1):
            for o in self.all:
                if o.region == r and o.is_dma:
                    tot[o.sem] += 16
            for k_ in tot:
                if k_.startswith("w1q"):
                    tot[k_] = 0
            reg_tot.append(dict(tot))
        block = es.enter_context(nc.Block())

        def run(e, name):
            known = {}
            for o in streams[name]:
                waits = {}
                for d in o.deps:
                    if d.is_dma:
                        s = dmasem[d.sem]
                        waits[s] = max(waits.get(s, 0), d.semval)
                    else:
                        s = engsem[d.eng]
                        waits[s] = max(waits.get(s, 0), d.ticket)
                if o in first_of:
                    for k, v in reg_tot[first_of[o] - 1].items():
                        if v:
                            s = dmasem[k]
                            waits[s] = max(waits.get(s, 0), v)
                for s, v in waits.items():
                    if known.get(s, 0) >= v:
                        continue
                    e.wait_ge(s, v)
                    known[s] = v
                if o.fn is None:
                    continue
                ins = o.fn(e)
                if isinstance(ins, (list, tuple)):
                    ins = ins[-1]
                if o.is_dma:
                    ins.then_inc(dmasem[o.sem], 16)
                elif o.signal:
                    ins.then_inc(engsem[name], 1)
            if name == "sp":
                for k, v in reg_tot[-1].items():
                    if v and known.get(dmasem[k], 0) < v:
                        e.wait_ge(dmasem[k], v)

        @block.tensor
        def _(e):
            run(e, "pe")

        @block.scalar
        def _(e):
            run(e, "act")

        @block.vector
        def _(e):
            run(e, "dve")

        @block.gpsimd
        def _(e):
            run(e, "pool")

        @block.sync
        def _(e):
            run(e, "sp")


class Arena:
    def __init__(self, ap_f32, nwords):
        self.t = ap_f32
        self.n = nwords
        self.off = 0

    def alloc(self, dtype, shape):
        n = 1
        for s in shape:
            n *= s
        words = n if dtype == F32 else (n + 1) // 2
        assert self.off + words <= self.n, f"arena overflow {self.off}+{words}>{self.n}"
        v = self.t[:, self.off:self.off + words]
        self.off += words
        if dtype == BF16:
            v = v.bitcast(BF16)
            if n % 2:
                v = v[:, 0:n]
        if len(shape) == 2:
            v = v.rearrange("p (a b) -> p a b", a=shape[0])
        elif len(shape) == 3:
            v = v.rearrange("p (a b c) -> p a b c", a=shape[0], b=shape[1])
        return v


def build_nc(T=SEQ, dbg=False, verbose=False):
    NB = T // 128
    NU = NB // 4
    assert NB % 4 == 0 and NU >= 1
    NI = T // 256

    nc = bass.Bass("TRN2", target_bir_lowering=False)

    def din(name, shape):
        return nc.dram_tensor(name, shape, F32, kind="ExternalInput").ap()

    x = din("x", [T, D])
    w_in = din("w_in", [D, INW])
    sink = din("sink", [1, NH])
    gmlp_ln_g = din("gmlp_ln_g", [1, 512])
    gmlp_ln_b = din("gmlp_ln_b", [1, 512])
    w_spatial = din("w_spatial", [8, 128, 128])
    b_spatial = din("b_spatial", [8, 128])
    attn_norm_g = din("attn_norm_g", [1, 512])
    gmlp_norm_g = din("gmlp_norm_g", [1, 512])
    w_out = din("w_out", [D, D])
    ln1_g = din("ln1_g", [1, D])
    ln1_b = din("ln1_b", [1, D])
    w_ff1 = din("w_ff1", [D, DFF])
    w_ff2 = din("w_ff2", [DFF, D])
    ln2_g = din("ln2_g", [1, D])
    ln2_b = din("ln2_b", [1, D])
    out = nc.dram_tensor("out", [T, D], F32, kind="ExternalOutput").ap()
    if dbg:
        x1s = nc.dram_tensor("x1s", [T, D], F32, kind="ExternalOutput").ap()
    else:
        x1s = nc.dram_tensor("x1s", [T, D], F32).ap()

    R = Rec()
    es = ExitStack()
    AW = 53200
    big = es.enter_context(nc.sbuf_tensor("arena", [128, AW], F32))
    psum = es.enter_context(nc.psum_tensor("psum", [128, 4096], F32))
    A = Arena(big, AW)

    def bank(b):
        return psum[:, b * 512:(b + 1) * 512]

    def small(name, n):
        return Buf(A.alloc(F32, [n]), name)

    def smalls(name, n, k=2):
        return [small(f"{name}{i}", n) for i in range(k)]

    ident = Buf(A.alloc(BF16, [128]), "ident")
    epsA = small("epsA", 1)
    epsB = small("epsB", 1)
    persist_off = A.off

    W1OFF = AW - 16384
    A.off = W1OFF
    w_in_all = A.alloc(BF16, [8, INW])
    w_in_q = Buf(w_in_all[:, :, 0:512], "w_in_q")
    w_in_kv = Buf(w_in_all[:, :, 512:768], "w_in_kv")
    w_in_gu = Buf(w_in_all[:, :, 768:1280], "w_in_gu")
    w_in_gv = Buf(w_in_all[:, :, 1280:1792], "w_in_gv")
    w_in_parts = [w_in_q, w_in_kv, w_in_gu, w_in_gv]
    A.off = persist_off
    wq_perm = Buf(A.alloc(BF16, [8, 512]), "wq_perm")
    w_out_all = A.alloc(BF16, [8, D])
    w_out_bf = [Buf(w_out_all[:, c, :], f"w_out_bf{c}") for c in range(8)]
    wsT = Buf(A.alloc(BF16, [8, 128]), "wsT")
    biasT2 = Buf(A.alloc(BF16, [4, 768]), "biasT2")
    LG = Buf(A.alloc(F32, [512]), "LG")
    LB = Buf(A.alloc(F32, [512]), "LB")
    G1 = Buf(A.alloc(F32, [D]), "G1")
    B1 = Buf(A.alloc(F32, [D]), "B1")
    esink = small("esink", NH)
    gcol = small("gcol", 8)
    bsT = small("bsT", 8)

    xb = [Buf(A.alloc(BF16, [D]), f"xb{i}") for i in range(2)]
    xres = [Buf(A.alloc(F32, [D]), f"xres{i}") for i in range(2)]
    xT_all = A.alloc(BF16, [8, 512])
    xTq = [Buf(xT_all[:, :, b * 128:(b + 1) * 128], f"xTq{b}") for b in range(4)]
    qz = [Buf(A.alloc(BF16, [4, 4, 256]), f"qz{i}") for i in range(2)]
    kT = [Buf(A.alloc(BF16, [512]), f"kT{i}") for i in range(2)]
    vaug_all = A.alloc(BF16, [8, 2, 66])
    vaug = [Buf(vaug_all[:, s_, :, 0:65], f"vaug{s_}") for s_ in range(8)]
    gu_sb = [Buf(A.alloc(F32, [512]), f"gu_sb{i}") for i in range(4)]
    gv_sb = [Buf(A.alloc(F32, [512]), f"gv_sb{i}") for i in range(4)]
    vn = [Buf(A.alloc(BF16, [512]), f"vn{i}") for i in range(2)]
    sgu_f = [Buf(A.alloc(F32, [512]), f"sgu_f{i}") for i in range(2)]
    junk1 = Buf(A.alloc(BF16, [512]), "junk")
    junk = [junk1, junk1]
    NSG = 6
    sgu_n = [Buf(A.alloc(BF16, [512]), f"sgu_n{i}") for i in range(NSG)]
    sS2 = [Buf(A.alloc(F32, [768]), f"sS2_{i}") for i in range(2)]
    pT2 = [Buf(A.alloc(BF16, [768]), f"pT2_{i}") for i in range(2)]
    attn_f = [Buf(A.alloc(F32, [512]), f"attn_f{i}") for i in range(2)]
    attn_n = [Buf(A.alloc(BF16, [512]), f"attn_n{i}") for i in range(2)]
    mixT = Buf(A.alloc(BF16, [8, 128]), "mixT")
    ybuf = [Buf(A.alloc(F32, [D]), f"y{i}") for i in range(3)]
    tmpA = Buf(ybuf[2].ap[:, 0:384], "tmpA", extra=[ybuf[2].lt])
    tmpB = Buf(ybuf[2].ap[:, 384:768], "tmpB", extra=[ybuf[2].lt])
    ws_bf = Buf(sS2[1].ap[:, 0:512].bitcast(BF16).rearrange("p (g s) -> p g s", g=8), "ws_bf", extra=[sS2[1].lt])
    st6 = smalls("st6", 6)
    mv = smalls("mv", 2)
    lnv = smalls("lnv", 1)
    rstd = smalls("rstd", 1)
    nmr = smalls("nmr", 1)
    ssq_s = smalls("ssq_s", 1)
    lnv_s = smalls("lnv_s", 1)
    rs_s = smalls("rs_s", 1)
    ssq_a = smalls("ssq_a", 1)
    lnv_a = smalls("lnv_a", 1)
    rs_a = smalls("rs_a", 1)
    den = smalls("den", 4, 4)
    rinv = smalls("rinv", 4, 4)
    st12 = smalls("st12", 12)
    mv1 = smalls("mv1", 2)
    lnv1 = smalls("lnv1", 1)
    rstd1 = smalls("rstd1", 1)
    nmr1 = smalls("nmr1", 1)
    phaseA_end = A.off
    assert phaseA_end <= W1OFF, (phaseA_end, W1OFF)
    w1q = []
    for q in range(4):
        ap = big[:, W1OFF + q * 4096:W1OFF + (q + 1) * 4096].bitcast(BF16).rearrange("p (c n) -> p c n", c=8)
        extra = [b_.lt for b_ in w_in_parts] if q < 2 else []
        w1q.append(Buf(ap, f"w1q{q}", extra=extra))

    def load_w1q(q):
        lq = (q + 2) % 4
        R.dma("pool", lambda e: e.dma_start(out=w1q[q].ap, in_=w_ff1[:, lq * 1024:(lq + 1) * 1024].rearrange(
            "(c p) n -> p c n", p=128)), [], [w1q[q]], f"w1q{q}", 4194304)

    psT_ap = bank(0).bitcast(BF16).rearrange("p (c t) -> p c t", c=8)
    BL = [LT(f"bank{b}") for b in range(8)]
    psT = Buf(psT_ap, "psT", extra=[BL[0]])
    psM = Buf(bank(0), "psM", extra=[BL[0]])
    psQ = [Buf(bank(1), "psQ0", extra=[BL[1]]), Buf(bank(2), "psQ1", extra=[BL[2]])]
    psP = [psQ[0], psQ[0]]
    psOlo = Buf(bank(2)[:, 0:260], "psOlo", extra=[BL[2]])
    psOhi = Buf(bank(3)[:, 0:260], "psOhi", extra=[BL[3]])
    Sb = [Buf(psum[:, 2048:2816], "S0", extra=[BL[4], BL[5]]), Buf(psum[:, 3072:3840], "S1", extra=[BL[6], BL[7]])]
    psG = [[Buf(bank(3), "psG0", extra=[BL[3]]), Buf(bank(4), "psG1", extra=[BL[4]])],
           [Buf(bank(5), "psG2", extra=[BL[5]]), Buf(bank(6), "psG3", extra=[BL[6]])]]
    psV = Buf(bank(7)[:, 384:512], "psV", extra=[BL[7]])

    slopes = [2.0 ** (-(h + 1)) for h in range(NH)]

    R.dve(lambda e: e.memset(epsA.ap, EPS), [], [epsA], 1)
    R.dve(lambda e: e.memset(epsB.ap, EPS / (ALPHA * ALPHA)), [], [epsB], 1)
    R.dve(lambda e: e.memset(vaug_all[:, :, :, 64:66], 1.0), [], vaug, 32)

    idf = tmpA.ap[:, 0:128]
    idn = tmpB.ap[:, 0:128]
    R.pool(lambda e: e.iota(idf, [[1, 128]], base=0, channel_multiplier=-1, allow_small_or_imprecise_dtypes=True),
           [], [tmpA], 128)
    R.dve(lambda e: e.tensor_scalar(out=idn, in0=idf, scalar1=-1.0, scalar2=None, op0=ALU.mult), [tmpA], [tmpB], 128)
    R.dve(lambda e: e.tensor_tensor(out=idf, in0=idf, in1=idn, op=ALU.max), [tmpA, tmpB], [tmpA], 128)
    R.dve(lambda e: e.tensor_scalar(out=idf, in0=idf, scalar1=1.0, scalar2=None, op0=ALU.min), [tmpA], [tmpA], 128)
    R.dve(lambda e: e.tensor_scalar(out=ident.ap, in0=idf, scalar1=-1.0, scalar2=1.0, op0=ALU.mult, op1=ALU.add),
          [tmpA], [ident], 128)

    dd = tmpA.ap
    R.pool(lambda e: e.iota(dd.rearrange("p (j q) -> p j q", j=3), [[128, 3], [-1, 128]], base=-128,
                            channel_multiplier=1, allow_small_or_imprecise_dtypes=True), [tmpA], [tmpA], 384)
    R.dve(lambda e: e.tensor_scalar(out=tmpB.ap, in0=dd, scalar1=-1.0, scalar2=None, op0=ALU.mult), [tmpA], [tmpB], 384)
    R.dve(lambda e: e.tensor_tensor(out=dd, in0=dd, in1=tmpB.ap, op=ALU.max), [tmpA, tmpB], [tmpA], 384)
    R.dve(lambda e: e.tensor_scalar(out=tmpB.ap, in0=dd, scalar1=-128.0, scalar2=0.0, op0=ALU.add, op1=ALU.max),
          [tmpA], [tmpB], 384)
    R.dve(lambda e: e.tensor_scalar(out=tmpB.ap, in0=tmpB.ap, scalar1=1.0, scalar2=8.0 * NEG, op0=ALU.min, op1=ALU.mult),
          [tmpB], [tmpB], 384)
    b2v = biasT2.ap.rearrange("p i (j h q) -> p i j h q", j=3, h=2)
    dd3 = dd.rearrange("p (j q) -> p j q", j=3)
    mk3 = tmpB.ap.rearrange("p (j q) -> p j q", j=3)
    for ic in range(4):
        for h2 in range(2):
            R.dve(lambda e, ic=ic, h2=h2: e.scalar_tensor_tensor(out=b2v[:, ic, :, h2, :], in0=dd3, scalar=-8.0 * slopes[ic + 4 * h2],
                                                                 in1=mk3, op0=ALU.mult, op1=ALU.add),
                  [tmpA, tmpB], [biasT2], 384)
    for i in range(2):
        R.pool(lambda e, i=i: e.iota(qz[i].ap.rearrange("p a b c -> p (a b c)"), [[0, 4096]], base=0, channel_multiplier=0,
                                     allow_small_or_imprecise_dtypes=True), [], [qz[i]], 2048)

    R.dma("pool", lambda e: e.dma_start(out=w_in_all, in_=w_in.rearrange("(c p) n -> p c n", p=128)),
          [], w_in_parts, "w_in", 128 * 8 * INW * 4)
    for ic in range(4):
        for h2 in range(2):
            hh = ic + 4 * h2
            R.dve(lambda e, ic=ic, h2=h2, hh=hh: e.tensor_copy(
                out=wq_perm.ap[:, :, ic * 128 + h2 * 64:ic * 128 + h2 * 64 + 64],
                in_=w_in_all[:, :, hh * 64:hh * 64 + 64]), [w_in_q], [wq_perm], 512)

    SETUP_LAT = [45000.0]

    def small_dma(key, dst, src, nb=4096, slow=False):
        if slow:
            R.dma("sp", lambda e: e.dma_start(out=dst_ap(dst), in_=src, allow_slow_non_contiguous=True), [], [dst[0]], key, nb,
                  lat=SETUP_LAT[0])
        else:
            R.dma("sp", lambda e: e.dma_start(out=dst_ap(dst), in_=src), [], [dst[0]], key, nb, lat=SETUP_LAT[0])

    def dst_ap(d):
        return d[1]

    small_dma("m_gc0", (gcol, gcol.ap[:, 0:4]), attn_norm_g.rearrange("o (c p) -> p (o c)", p=128), slow=True)
    small_dma("m_gc1", (gcol, gcol.ap[:, 4:8]), gmlp_norm_g.rearrange("o (c p) -> p (o c)", p=128), slow=True)
    small_dma("m_bs", (bsT, bsT.ap), b_spatial.rearrange("g t -> t g"), slow=True)
    small_dma("m_lg", (LG, LG.ap), gmlp_ln_g.partition_broadcast(128), 262144)
    small_dma("m_lb", (LB, LB.ap), gmlp_ln_b.partition_broadcast(128), 262144)
    small_dma("m_g1", (G1, G1.ap), ln1_g.partition_broadcast(128), 524288)
    small_dma("m_b1", (B1, B1.ap), ln1_b.partition_broadcast(128), 524288)
    small_dma("m_sk", (esink, esink.ap), sink.partition_broadcast(128))
    R.act(lambda e: e.activation(out=esink.ap, in_=esink.ap, func=AF.Exp), [esink], [esink], 8, "el")
    for c in range(8):
        st = ybuf[c % 2]
        R.dma("sp", lambda e, c=c, st=st: e.dma_start(out=st.ap, in_=w_out[c * 128:(c + 1) * 128, :]),
              [], [st], f"stage{c % 2}", 524288, lat=(55000.0 if c < 2 else 5000.0))
        R.dve(lambda e, c=c, st=st: e.tensor_scalar(out=w_out_bf[c].ap, in0=st.ap, scalar1=gcol.ap[:, c:c + 1],
                                                     scalar2=None, op0=ALU.mult), [st, gcol], [w_out_bf[c]], 1024)
    wsf = ybuf[2]
    rowsum = small("rowsum", 8)
    R.dma("sp", lambda e: e.dma_start(out=wsf.ap.rearrange("p (g s) -> p g s", g=8), in_=w_spatial.rearrange("g t s -> t g s")),
          [], [wsf], "m_wsf", 524288, lat=60000.0)
    def fold_lb():
        R.dve(lambda e: e.tensor_reduce(out=rowsum.ap, in_=wsf.ap.rearrange("p (g s) -> p g s", g=8), axis=mybir.AxisListType.X,
                                        op=ALU.add), [wsf, wq_perm, qz[0]], [rowsum], 1024)
        for g in range(8):
            R.dve(lambda e, g=g: e.tensor_scalar(out=LB.ap[:, g * 64:(g + 1) * 64], in0=LB.ap[:, g * 64:(g + 1) * 64],
                                                 scalar1=rowsum.ap[:, g:g + 1], scalar2=bsT.ap[:, g:g + 1],
                                                 op0=ALU.mult, op1=ALU.add), [LB, rowsum, bsT], [LB], 64)

    R.dma("pool", lambda e: e.dma_start(out=ws_bf.ap, in_=w_spatial.rearrange("g t s -> t g s")),
          [], [ws_bf], "ws", 524288)
    R.pe(lambda e: [e.transpose(psT.ap[:, g, :], ws_bf.ap[:, g, :], ident.ap) for g in range(8)],
         [ws_bf, ident], [psT], [128] * 8)
    R.dve(lambda e: e.tensor_copy(out=wsT.ap, in_=psT.ap), [psT], [wsT], 1024)

    def rsqrt_act(src_ap, src_buf, eps_buf, scale, lnb, outb):
        R.act(lambda e: e.activation(out=lnb.ap, in_=src_ap, func=AF.Ln, bias=eps_buf.ap, scale=scale),
              [src_buf, eps_buf], [lnb], 1, "el")
        R.act(lambda e: e.activation(out=outb.ap, in_=lnb.ap, func=AF.Exp, scale=-0.5), [lnb], [outb], 1, "el")

    def front(u):
        for b in range(4):
            blk = 4 * u + b
            xbb = xb[blk % 2]
            R.dma("pool", lambda e, blk=blk, xbb=xbb: e.dma_start(out=xbb.ap, in_=x[blk * 128:(blk + 1) * 128, :]),
                  [], [xbb], f"xb{blk % 2}", 524288)
            R.pe(lambda e, xbb=xbb: [e.transpose(psT.ap[:, c, :], xbb.ap[:, c * 128:(c + 1) * 128], ident.ap)
                                     for c in range(8)], [xbb, ident], [psT], [128] * 8)
            R.dve(lambda e, b=b: e.tensor_copy(out=xTq[b].ap, in_=psT.ap), [psT], [xTq[b]], 1024)
        for grp in range(5):
            pq = psQ[grp % 2]
            if grp < 4:
                wsrc, rd, c0 = wq_perm.ap, [wq_perm], grp * 128
            else:
                wsrc, rd, c0 = w_in_all, [w_in_kv], 512
            R.pe(lambda e, wsrc=wsrc, c0=c0, pq=pq: [
                e.matmul(pq.ap, lhsT=wsrc[:, c, c0:c0 + 128], rhs=xT_all[:, c, :], start=(c == 0), stop=(c == 7))
                for c in range(8)], rd + xTq, [pq], [512] * 8)
            if grp < 4:
                dst = qz[u % 2]
                R.dve(lambda e, grp=grp, pq=pq, dst=dst: e.tensor_copy(
                    out=dst.ap[0:64, grp, :, 0:128], in_=pq.ap[0:64, :].rearrange("p (b q) -> p b q", b=4)), [pq], [dst], 512)
                R.act(lambda e, grp=grp, pq=pq, dst=dst: e.activation(
                    out=dst.ap[64:128, grp, :, 128:256], in_=pq.ap[64:128, :].rearrange("p (b q) -> p b q", b=4),
                    func=AF.Copy), [pq], [dst], 512)
            else:
                dst = kT[u % 2]
                R.act(lambda e, pq=pq, dst=dst: e.activation(out=dst.ap, in_=pq.ap, func=AF.Copy), [pq], [dst], 512)
        gel = []
        for b in range(4):
            blk = 4 * u + b
            xs = xTq[b]
            R.pe(lambda e, xs=xs: [e.matmul(psV.ap, lhsT=xs.ap[:, c, :], rhs=w_in_all[:, c, 640:768],
                                            start=(c == 0), stop=(c == 7)) for c in range(8)],
                 [xs, w_in_kv], [psV], [128] * 8)
            va = vaug[blk % 8]
            R.dve(lambda e, va=va: e.tensor_copy(out=va.ap[:, :, 0:64], in_=psV.ap.rearrange("p (g d) -> p g d", g=2)),
                  [psV], [va], 128)
            pgu, pgv = psG[b % 2]
            R.pe(lambda e, xs=xs, pgu=pgu: [e.matmul(pgu.ap, lhsT=xs.ap[:, c, :], rhs=w_in_all[:, c, 768:1280],
                                                     start=(c == 0), stop=(c == 7)) for c in range(8)],
                 [xs, w_in_gu], [pgu], [512] * 8)
            gel.append((gu_sb[b], pgu))
            R.pe(lambda e, xs=xs, pgv=pgv: [e.matmul(pgv.ap, lhsT=xs.ap[:, c, :], rhs=w_in_all[:, c, 1280:1792],
                                                     start=(c == 0), stop=(c == 7)) for c in range(8)],
                 [xs, w_in_gv], [pgv], [512] * 8)
            gel.append((gv_sb[b], pgv))
            if b % 2 == 1:
                srcs = [g_[1] for g_ in gel]
                for (dst_, src_) in gel:
                    R.act(lambda e, dst_=dst_, src_=src_: e.activation(out=dst_.ap, in_=src_.ap, func=AF.Gelu),
                          srcs, [dst_], 512, "gelu")
                gel = []

    def sgu_vec(b, blk):
        p = blk % 2
        g = gv_sb[b]
        R.dve(lambda e: e.bn_stats(out=st6[p].ap, in_=g.ap), [g], [st6[p]], 512)
        R.dve(lambda e: e.bn_aggr(out=mv[p].ap, in_=st6[p].ap), [st6[p]], [mv[p]], 26)
        rsqrt_act(mv[p].ap[:, 1:2], mv[p], epsA, 1.0, lnv[p], rstd[p])
        R.dve(lambda e: e.scalar_tensor_tensor(out=nmr[p].ap, in0=mv[p].ap[:, 0:1], scalar=-1.0, in1=rstd[p].ap,
                                               op0=ALU.mult, op1=ALU.mult), [mv[p], rstd[p]], [nmr[p]], 1)
        R.act(lambda e: e.activation(out=vn[p].ap, in_=g.ap, func=AF.Identity, bias=nmr[p].ap, scale=rstd[p].ap),
              [g, nmr[p], rstd[p]], [vn[p]], 512)

    def sgu_mm(b, blk):
        p = blk % 2
        R.pe(lambda e: [e.matmul(psM.ap[:, g * 64:(g + 1) * 64], lhsT=wsT.ap[:, g, :], rhs=vn[p].ap[:, g * 64:(g + 1) * 64],
                                 start=True, stop=True) for g in range(8)], [wsT, vn[p]], [psM], [128] * 8)
        R.dve(lambda e: e.tensor_tensor(out=sgu_f[p].ap, in0=psM.ap, in1=LG.ap, op=ALU.mult), [psM, LG], [sgu_f[p]], 512)
        R.dve(lambda e: e.tensor_tensor(out=sgu_f[p].ap, in0=sgu_f[p].ap, in1=LB.ap, op=ALU.add), [sgu_f[p], LB], [sgu_f[p]], 512)
        R.dve(lambda e: e.tensor_tensor(out=sgu_f[p].ap, in0=sgu_f[p].ap, in1=gu_sb[b].ap, op=ALU.mult),
              [sgu_f[p], gu_sb[b]], [sgu_f[p]], 512)
        R.act(lambda e: e.activation(out=junk[0].ap, in_=sgu_f[p].ap, func=AF.Square, accum_out=ssq_s[p].ap),
              [sgu_f[p]], [junk[0], ssq_s[p]], 512)
        rsqrt_act(ssq_s[p].ap, ssq_s[p], epsA, 1.0 / 512.0, lnv_s[p], rs_s[p])
        dst = sgu_n[blk % NSG]
        R.act(lambda e: e.activation(out=dst.ap, in_=sgu_f[p].ap, func=AF.Identity, scale=rs_s[p].ap),
              [sgu_f[p], rs_s[p]], [dst], 512)

    def attention(m):
        p = m % 2
        js = [j for j in range(3) if 0 <= m - 1 + j < NB]
        lo, hi = js[0] * 256, (js[-1] + 1) * 256
        qs = qz[(m // 4) % 2]
        qb = m % 4
        af = attn_f[p]
        for ic in range(4):
            pS = Sb[ic % 2]
            sSb = sS2[ic % 2]
            pTb = pT2[ic % 2]
            kks = [(j, kT[((m - 1 + j) // 4) % 2], ((m - 1 + j) % 4) * 128) for j in js]
            def s_mms(e, kks=kks, ic=ic, pS=pS):
                r = []
                for (j, ks, koff) in kks:
                    r.append(e.matmul(pS.ap[:, j * 256:(j + 1) * 256], lhsT=ks.ap[:, koff:koff + 128],
                                      rhs=qs.ap[:, ic, qb, :], start=True, stop=False))
                    r.append(e.matmul(pS.ap[:, j * 256:(j + 1) * 256], lhsT=ident.ap,
                                      rhs=biasT2.ap[:, ic, j * 256:(j + 1) * 256], start=False, stop=True))
                return r
            R.pe(s_mms, [k[1] for k in kks] + [qs, biasT2, ident], [pS], [256] * (2 * len(js)))
            R.act(lambda e, pS=pS, pTb=pTb: e.activation(out=pTb.ap[:, lo:hi], in_=pS.ap[:, lo:hi], func=AF.Exp, scale=0.125),
                  [pS], [pTb], hi - lo, "el")
            vas = [(j, vaug[(m - 1 + j) % 8]) for j in js]
            for h2 in range(2):
                pO = psOlo if h2 == 0 else psOhi
                R.pe(lambda e, vas=vas, h2=h2, ic=ic, pO=pO, pTb=pTb: [
                    e.matmul(pO.ap[:, ic * 65:ic * 65 + 65], lhsT=pTb.ap[:, j * 256 + h2 * 128:j * 256 + h2 * 128 + 128],
                             rhs=va.ap[:, h2, :], start=(j == js[0]), stop=(j == js[-1])) for (j, va) in vas],
                     [pTb] + [v[1] for v in vas], [pO], [128] * len(js))
        for half in range(2):
            pO = psOlo if half == 0 else psOhi
            o3 = pO.ap.rearrange("p (h d) -> p h d", h=4)
            dn, ri = den[2 * p + half], rinv[2 * p + half]
            R.dve(lambda e, o3=o3, dn=dn, half=half: e.tensor_tensor(
                out=dn.ap, in0=o3[:, :, 64], in1=esink.ap[:, half * 4:(half + 1) * 4], op=ALU.add),
                  [pO, esink], [dn], 4)
            R.dve(lambda e, dn=dn, ri=ri: e.reciprocal(out=ri.ap, in_=dn.ap), [dn], [ri], 4)
            R.dve(lambda e, o3=o3, ri=ri, half=half: e.tensor_tensor(
                out=af.ap[:, half * 256:(half + 1) * 256].rearrange("p (h d) -> p h d", h=4),
                in0=o3[:, :, 0:64], in1=ri.ap.unsqueeze(2).to_broadcast([128, 4, 64]), op=ALU.mult),
                  [pO, ri], [af], 256)
        R.act(lambda e: e.activation(out=junk[1].ap, in_=af.ap, func=AF.Square, accum_out=ssq_a[p].ap),
              [af], [junk[1], ssq_a[p]], 512)
        rsqrt_act(ssq_a[p].ap, ssq_a[p], epsA, 1.0 / 512.0, lnv_a[p], rs_a[p])
        R.act(lambda e: e.activation(out=attn_n[p].ap, in_=af.ap, func=AF.Identity, scale=rs_a[p].ap),
              [af, rs_a[p]], [attn_n[p]], 512)

    def mix(m):
        p = m % 2
        p3 = m % 3
        xr = xres[p]
        R.dma("sp", lambda e: e.dma_start(out=xr.ap, in_=x[m * 128:(m + 1) * 128, :]), [], [xr], f"xres{p}", 524288)
        sg = sgu_n[m % NSG]
        an = attn_n[p]
        R.pe(lambda e: [e.transpose(psT.ap[:, c, :], (an.ap[:, c * 128:(c + 1) * 128] if c < 4 else
                                                      sg.ap[:, (c - 4) * 128:(c - 3) * 128]), ident.ap) for c in range(8)],
             [an, sg, ident], [psT], [128] * 8)
        R.dve(lambda e: e.tensor_copy(out=mixT.ap, in_=psT.ap), [psT], [mixT], 1024)
        yb = ybuf[p3]
        for hf in range(2):
            pp = psP[hf]
            R.pe(lambda e, hf=hf, pp=pp: [e.matmul(pp.ap, lhsT=mixT.ap[:, c, :], rhs=w_out_bf[c].ap[:, hf * 512:(hf + 1) * 512],
                                                   start=(c == 0), stop=(c == 7)) for c in range(8)],
                 [mixT] + w_out_bf, [pp], [512] * 8)
            R.dve(lambda e, hf=hf, pp=pp: e.scalar_tensor_tensor(
                out=yb.ap[:, hf * 512:(hf + 1) * 512], in0=pp.ap, scalar=1.0 / ALPHA,
                in1=xr.ap[:, hf * 512:(hf + 1) * 512], op0=ALU.mult, op1=ALU.add), [pp, xr], [yb], 512)
        for hf in range(2):
            R.dve(lambda e, hf=hf: e.bn_stats(out=st12[p].ap[:, hf * 6:(hf + 1) * 6], in_=yb.ap[:, hf * 512:(hf + 1) * 512]),
                  [yb], [st12[p]], 512)
        R.dve(lambda e: e.bn_aggr(out=mv1[p].ap, in_=st12[p].ap), [st12[p]], [mv1[p]], 52)
        rsqrt_act(mv1[p].ap[:, 1:2], mv1[p], epsB, 1.0, lnv1[p], rstd1[p])
        R.dve(lambda e: e.scalar_tensor_tensor(out=nmr1[p].ap, in0=mv1[p].ap[:, 0:1], scalar=-1.0, in1=rstd1[p].ap,
                                               op0=ALU.mult, op1=ALU.mult), [mv1[p], rstd1[p]], [nmr1[p]], 1)
        R.act(lambda e: e.activation(out=yb.ap, in_=yb.ap, func=AF.Identity, bias=nmr1[p].ap, scale=rstd1[p].ap),
              [yb, nmr1[p], rstd1[p]], [yb], 1024)
        for (cst, aop) in ((G1, ALU.mult), (B1, ALU.add)):
            for qq in range(4):
                R.pool(lambda e, cst=cst, aop=aop, qq=qq: e.tensor_tensor(
                    out=yb.ap[:, qq * 256:(qq + 1) * 256], in0=yb.ap[:, qq * 256:(qq + 1) * 256],
                    in1=cst.ap[:, qq * 256:(qq + 1) * 256], op=aop), [yb, cst], [yb], 256)
        R.dma("sp", lambda e: e.dma_start(out=x1s[m * 128:(m + 1) * 128, :], in_=yb.ap), [yb], [], f"x1t{p3}", 524288)

    for u in range(NU + 1):
        if u < NU:
            R.cur_boost = 1e8 if u == 0 else 0.0
            front(u)
            R.cur_boost = 0.0
        if u == 0:
            fold_lb()
        if u == max(NU - 2, 0):
            load_w1q(2)
            load_w1q(3)
        if u == NU - 1:
            load_w1q(0)
            load_w1q(1)
        for i in range(4):
            blk = 4 * u + i
            m = 4 * u - 1 + i
            do_sgu = u < NU
            do_att = (0 <= m < NB) and (u < NU or i == 0)
            if do_sgu:
                sgu_vec(i, blk)
            if do_att:
                attention(m)
            if do_sgu:
                sgu_mm(i, blk)
            if do_att:
                mix(m)

    R.barrier()
    A.off = persist_off
    w2_all = A.alloc(BF16, [32, D])
    w2q = [Buf(w2_all[:, q * 8:(q + 1) * 8, :], f"w2q{q}") for q in range(4)]
    G2 = Buf(A.alloc(F32, [D]), "G2")
    B2 = Buf(A.alloc(F32, [D]), "B2")
    hT_all = A.alloc(BF16, [32, 256])
    hT = [Buf(hT_all[:, fc, :], f"hT{fc}") for fc in range(32)]
    x1 = [Buf(A.alloc(F32, [2, D]), f"x1_{i}") for i in range(2)]
    x1b = [Buf(A.alloc(BF16, [2, D]), f"x1b{i}") for i in range(2)]
    x1T_all = [A.alloc(BF16, [8, 256]) for i in range(2)]
    x1T = [[Buf(x1T_all[i][:, :, s_ * 128:(s_ + 1) * 128], f"x1T{i}_{s_}") for s_ in range(2)] for i in range(2)]
    rbuf = [Buf(A.alloc(F32, [256]), f"rbuf{i}") for i in range(3)]
    y2 = [Buf(A.alloc(F32, [2, D]), f"y2_{i}") for i in range(2)]
    stB = smalls("stB", 12)
    mvB = smalls("mvB", 2)
    lnvB = smalls("lnvB", 1)
    rstdB = smalls("rstdB", 1)
    nmrB = smalls("nmrB", 1)
    phaseB_end = A.off
    assert phaseB_end <= W1OFF, (phaseB_end, W1OFF)
    if verbose:
        print("arena words: phaseA", phaseA_end, "phaseB", phaseB_end, "of", AW)

    psTB = Buf(psT_ap, "psTB", extra=[BL[0]])
    psH = [Buf(bank(1 + i)[:, 0:256], f"psH{i}", extra=[BL[1 + i]]) for i in range(3)]
    psY = [[Buf(bank(4 + 2 * s_ + hf), f"psY{s_}{hf}", extra=[BL[4 + 2 * s_ + hf]]) for hf in range(2)] for s_ in range(2)]

    def load_w2():
        for q in range(4):
            R.dma("pool", lambda e, q=q: e.dma_start(out=w2q[q].ap, in_=w_ff2[q * 1024:(q + 1) * 1024, :].rearrange(
                "(c p) n -> p c n", p=128)), [x1b[0]], [w2q[q]], f"w2q{q}", 4194304)
    SETUP_LAT[0] = None
    small_dma("m_g2", (G2, G2.ap), ln2_g.partition_broadcast(128), 524288)
    small_dma("m_b2", (B2, B2.ap), ln2_b.partition_broadcast(128), 524288)

    def b_load(i):
        xs = x1[i % 2]
        xbb = x1b[i % 2]
        src = x1s[i * 256:(i + 1) * 256, :].rearrange("(s p) d -> p s d", p=128)
        R.dma("sp", lambda e: e.dma_start(out=xs.ap, in_=src), [], [xs], f"x1L{i % 2}", 1048576)
        R.dma("pool", lambda e: e.dma_start(out=xbb.ap, in_=src), [], [xbb], f"x1bL{i % 2}", 1048576)

    def b_transpose(i):
        xbb = x1b[i % 2]
        for s_ in range(2):
            R.pe(lambda e, s_=s_: [e.transpose(psTB.ap[:, c, :], xbb.ap[:, s_, c * 128:(c + 1) * 128], ident.ap)
                                   for c in range(8)], [xbb, ident], [psTB], [128] * 8)
            dst = x1T[i % 2][s_]
            R.dve(lambda e, dst=dst: e.tensor_copy(out=dst.ap, in_=psTB.ap), [psTB], [dst], 1024)

    def b_ff1(i):
        xt = x1T_all[i % 2]
        for fc in range(32):
            ph = psH[fc % 3]
            rb = rbuf[fc % 3]
            R.pe(lambda e, fc=fc, ph=ph: [e.matmul(ph.ap, lhsT=w1q[(fc // 8 + 2) % 4].ap[:, c, (fc % 8) * 128:(fc % 8 + 1) * 128], rhs=xt[:, c, :],
                                                   start=(c == 0), stop=(c == 7)) for c in range(8)],
                 [w1q[(fc // 8 + 2) % 4]] + x1T[i % 2], [ph], [256] * 8)
            R.act(lambda e, ph=ph, rb=rb: e.activation(out=rb.ap, in_=ph.ap, func=AF.Relu), [ph], [rb], 256)
            R.dve(lambda e, fc=fc, rb=rb: e.tensor_tensor(out=hT[fc].ap, in0=rb.ap, in1=rb.ap, op=ALU.mult),
                  [rb], [hT[fc]], 256)

    def b_ff2(i):
        xs = x1[i % 2]
        yo = y2[i % 2]
        for s_ in range(2):
            for q in range(4):
                R.pe(lambda e, s_=s_, q=q: [e.matmul(psY[s_][hf].ap, lhsT=hT[fc].ap[:, s_ * 128:(s_ + 1) * 128],
                                                     rhs=w2_all[:, fc, hf * 512:(hf + 1) * 512],
                                                     start=(fc == 0), stop=(fc == 31))
                                            for fc in range(q * 8, q * 8 + 8) for hf in range(2)],
                     hT[q * 8:q * 8 + 8] + [w2q[q]], psY[s_], [512] * 16)
        for s_ in range(2):
            p = s_
            for hf in range(2):
                py = psY[s_][hf]
                R.dve(lambda e, s_=s_, hf=hf, py=py: e.scalar_tensor_tensor(
                    out=yo.ap[:, s_, hf * 512:(hf + 1) * 512], in0=py.ap, scalar=1.0 / ALPHA,
                    in1=xs.ap[:, s_, hf * 512:(hf + 1) * 512], op0=ALU.mult, op1=ALU.add), [py, xs], [yo], 512)
            for hf in range(2):
                R.dve(lambda e, s_=s_, hf=hf, p=p: e.bn_stats(out=stB[p].ap[:, hf * 6:(hf + 1) * 6],
                                                         in_=yo.ap[:, s_, hf * 512:(hf + 1) * 512]), [yo], [stB[p]], 512)
            R.dve(lambda e, p=p: e.bn_aggr(out=mvB[p].ap, in_=stB[p].ap), [stB[p]], [mvB[p]], 52)
            rsqrt_act(mvB[p].ap[:, 1:2], mvB[p], epsB, 1.0, lnvB[p], rstdB[p])
            R.dve(lambda e, p=p: e.scalar_tensor_tensor(out=nmrB[p].ap, in0=mvB[p].ap[:, 0:1], scalar=-1.0, in1=rstdB[p].ap,
                                                        op0=ALU.mult, op1=ALU.mult), [mvB[p], rstdB[p]], [nmrB[p]], 1)
            R.act(lambda e, s_=s_, p=p: e.activation(out=yo.ap[:, s_, :], in_=yo.ap[:, s_, :], func=AF.Identity,
                                                     bias=nmrB[p].ap, scale=rstdB[p].ap), [yo, nmrB[p], rstdB[p]], [yo], 1024)
            R.pool(lambda e, s_=s_: e.tensor_tensor(out=yo.ap[:, s_, :], in0=yo.ap[:, s_, :], in1=G2.ap, op=ALU.mult),
                   [yo, G2], [yo], 1024)
            R.pool(lambda e, s_=s_: e.tensor_tensor(out=yo.ap[:, s_, :], in0=yo.ap[:, s_, :], in1=B2.ap, op=ALU.add),
                   [yo, B2], [yo], 1024)
        R.dma("sp", lambda e: e.dma_start(out=out[i * 256:(i + 1) * 256, :].rearrange("(s p) d -> p s d", p=128), in_=yo.ap),
              [yo], [], f"out{i % 2}", 1048576)

    b_load(0)
    load_w2()
    b_transpose(0)
    for i in range(NI):
        if i + 1 < NI:
            b_load(i + 1)
        b_ff1(i)
        if i + 1 < NI:
            b_transpose(i + 1)
        b_ff2(i)

    R.emit(nc, es)
    if verbose:
        print("sim times us", [t / 1e3 for t in R.sim_times], "busy", R.sim_busy, "ops", len(R.all))
    es.close()
    return nc


_WNAMES = ["w_in", "sink", "gmlp_ln_g", "gmlp_ln_b", "w_spatial", "b_spatial", "attn_norm_g", "gmlp_norm_g",
           "w_out", "ln1_g", "ln1_b", "w_ff1", "w_ff2", "ln2_g", "ln2_b"]


def kernel(**inputs):
    x = np.ascontiguousarray(np.asarray(inputs["x"], dtype=np.float32))
    B, S, _ = x.shape
    shared = {}
    for n in _WNAMES:
        a = np.asarray(inputs[n], dtype=np.float32)
        a = a[0]
        if a.ndim == 1:
            a = a[None, :]
        shared[n] = np.ascontiguousarray(a)
    nc = build_nc(S)
    in_maps = []
    for b in range(B):
        m = {"x": x[b]}
        m.update(shared)
        in_maps.append(m)
    res = run_bass_kernel_spmd(nc, in_maps, core_ids=list(range(B)))
    return np.stack([np.asarray(r["out"], dtype=np.float32) for r in res.results], axis=0)
```

```python
import numpy as np
from contextlib import ExitStack

import concourse.bass as bass
import concourse.mybir as mybir
from concourse.bass_utils import run_bass_kernel_spmd

F32 = mybir.dt.float32
BF16 = mybir.dt.bfloat16
AF = mybir.ActivationFunctionType
ALU = mybir.AluOpType

D = 1024
DFF = 4096
INW = 1792
NH = 8
ALPHA = 2.0 ** 0.25
EPS = 1e-5
NEG = -30000.0
N_CORES = 8
SEQ = 8192

ENGS = ("pe", "act", "dve", "pool", "sp")


class LT:
    __slots__ = ("name", "w", "rs")

    def __init__(self, name):
        self.name = name
        self.w = None
        self.rs = []


class Buf:
    __slots__ = ("ap", "lt", "extra")

    def __init__(self, ap, name, extra=()):
        self.ap = ap
        self.lt = LT(name)
        self.extra = list(extra)


class Op:
    __slots__ = ("eng", "fn", "alldeps", "deps", "signal", "ticket", "is_dma", "sem", "semval", "idx", "dur",
                 "tbl", "region", "lat", "fin", "ndeps", "users", "boost")


def _d_pe(cols):
    return sum(max(c, 128) * 0.45 + 20 for c in cols)


def _d_dve(n):
    return (n + 151) / 0.96


def _d_act(n):
    return 210 + 0.95 * n


def _d_pool(n):
    return 250 + 2.2 * n


class Rec:
    def __init__(self):
        self.all = []
        self.region = 0
        self.dma_keys = {}
        self.cur_boost = 0.0

    def _lts(self, xs):
        r = []
        for x in xs:
            if isinstance(x, Buf):
                r.append(x.lt)
                r.extend(x.extra)
            else:
                r.append(x)
        return r

    def _new(self, eng, fn, reads, writes, dur, is_dma=False, tbl=None, sem=None, lat=0.0):
        o = Op()
        o.eng = eng
        o.fn = fn
        o.is_dma = is_dma
        o.signal = False
        o.ticket = None
        o.sem = sem
        o.semval = None
        o.dur = dur
        o.tbl = tbl
        o.lat = lat
        o.boost = self.cur_boost
        o.region = self.region
        o.idx = len(self.all)
        deps = set()
        reads = self._lts(reads)
        writes = self._lts(writes)
        for t in reads:
            if t.w is not None:
                deps.add(t.w)
        for t in writes:
            if t.w is not None:
                deps.add(t.w)
            deps.update(t.rs)
        deps.discard(o)
        o.alldeps = deps
        for t in reads:
            t.rs.append(o)
        for t in writes:
            t.w = o
            t.rs = []
        self.all.append(o)
        return o

    def pe(self, fns, reads, writes, cols):
        return self._new("pe", fns, reads, writes, _d_pe(cols))

    def dve(self, fn, reads, writes, n):
        return self._new("dve", fn, reads, writes, _d_dve(n))

    def act(self, fn, reads, writes, n, tbl=None):
        return self._new("act", fn, reads, writes, _d_act(n), tbl=tbl)

    def pool(self, fn, reads, writes, n):
        return self._new("pool", fn, reads, writes, _d_pool(n))

    def dma(self, queue, fn, reads, writes, semkey, nbytes, lat=None):
        assert self.dma_keys.setdefault(semkey, queue) == queue
        issue = 120.0 if queue == "sp" else 1200.0
        if lat is None:
            lat = 2000.0 + nbytes / 150.0
        return self._new(queue, fn, reads, writes, issue, is_dma=True, sem=semkey, lat=lat)

    def barrier(self):
        self.region += 1

    def schedule(self):
        import heapq
        streams = {e: [] for e in ENGS}
        nreg = self.region + 1
        bar = []
        HOP = 100.0
        TBL = 1400.0
        for reg in range(nreg):
            ops = [o for o in self.all if o.region == reg]
            for o in ops:
                o.users = []
                o.ndeps = 0
            for o in ops:
                for d in o.alldeps:
                    if d.region == reg:
                        d.users.append(o)
                        o.ndeps += 1
            tail = {}
            for o in reversed(ops):
                t = 0.0
                for u in o.users:
                    tu = tail[u]
                    if tu > t:
                        t = tu
                tail[o] = t + o.dur + o.lat + HOP + o.boost
            efree = {e: 0.0 for e in ENGS}
            tblstate = [None]
            wait_h = {e: [] for e in ENGS}
            rdy_h = {e: [] for e in ENGS}
            ready_t = {}
            for o in ops:
                if o.ndeps == 0:
                    heapq.heappush(wait_h[o.eng], (0.0, o.idx, o))
                    ready_t[o] = 0.0
            nleft = len(ops)
            local = {e: [] for e in ENGS}
            while nleft:
                best = None
                for e in ENGS:
                    wh, rh = wait_h[e], rdy_h[e]
                    while wh and wh[0][0] <= efree[e]:
                        t, i, o = heapq.heappop(wh)
                        heapq.heappush(rh, (-tail[o], i, o))
                    cand = None
                    if rh:
                        if e == "act":
                            pick = None
                            for ent in heapq.nsmallest(16, rh):
                                o = ent[2]
                                if o.tbl is None or o.tbl == tblstate[0]:
                                    pick = ent
                                    break
                            if pick is None:
                                pick = rh[0]
                                st = efree[e] + TBL
                            else:
                                st = efree[e]
                            cand = (st, pick[0], pick[1], pick[2], pick)
                        else:
                            ent = rh[0]
                            cand = (efree[e], ent[0], ent[1], ent[2], ent)
                    elif wh:
                        t, i, o = wh[0]
                        st = t
                        if e == "act" and o.tbl is not None and o.tbl != tblstate[0]:
                            st = max(t, efree[e] + TBL)
                        cand = (st, -tail[o], i, o, None)
                    if cand is not None and (best is None or cand[:3] < best[:3]):
                        best = cand + (e,)
                st, _, i, o, ent, e = best
                if ent is not None:
                    rdy_h[e].remove(ent)
                    heapq.heapify(rdy_h[e])
                else:
                    heapq.heappop(wait_h[e])
                if e == "act" and o.tbl is not None:
                    tblstate[0] = o.tbl
                efree[e] = st + o.dur
                o.fin = st + o.dur + o.lat + HOP
                local[e].append(o)
                nleft -= 1
                for u in o.users:
                    u.ndeps -= 1
                    ready_t[u] = max(ready_t.get(u, 0.0), o.fin)
                    if u.ndeps == 0:
                        heapq.heappush(wait_h[u.eng], (ready_t[u], u.idx, u))
            for e in ENGS:
                streams[e].extend(local[e])
            bar.append({e: (local[e][-1] if local[e] else None) for e in ENGS})
            self.sim_time = max(efree.values())
            self.sim_times = getattr(self, "sim_times", []) + [self.sim_time]
            self.sim_busy = getattr(self, "sim_busy", []) + [{e: round(sum(o.dur for o in local[e]) / 1e3) for e in ENGS}]
        return streams, bar

    def emit(self, nc, es):
        streams, bar = self.schedule()
        first_of = {}
        for e in ENGS:
            seen = set()
            for o in streams[e]:
                if o.region > 0 and (e, o.region) not in seen:
                    seen.add((e, o.region))
                    first_of[o] = o.region
        for e in ENGS:
            for o in streams[e]:
                deps = set()
                for d in o.alldeps:
                    if d.is_dma:
                        deps.add(d)
                    elif d.eng == "pe" and o.eng == "pe":
                        continue
                    else:
                        d.signal = True
                        deps.add(d)
                o.deps = deps
        for o, reg in first_of.items():
            for r in range(reg):
                for e2, lo in bar[r].items():
                    if lo is not None and not lo.is_dma and not (lo.eng == "pe" and o.eng == "pe"):
                        lo.signal = True
                        o.deps.add(lo)
        engsem = {e: es.enter_context(nc.semaphore("c_" + e)) for e in ENGS if e != "sp"}
        dmasem = {k: es.enter_context(nc.semaphore("d_" + k)) for k in self.dma_keys}
        dcount = {k: 0 for k in self.dma_keys}
        region_dma_tot = {}
        for e in ENGS:
            k = 0
            for o in streams[e]:
                if o.is_dma:
                    dcount[o.sem] += 16
                    o.semval = dcount[o.sem]
                elif o.signal:
                    k += 1
                    o.ticket = k
        tot = {k: 0 for k in self.dma_keys}
        reg_tot = []
        for r in range(self.region + 1):
            for o in self.all:
                if o.region == r and o.is_dma:
                    tot[o.sem] += 16
            for k_ in tot:
                if k_.startswith("w1q"):
                    tot[k_] = 0
            reg_tot.append(dict(tot))
        block = es.enter_context(nc.Block())

        def run(e, name):
            known = {}
            for o in streams[name]:
                waits = {}
                for d in o.deps:
                    if d.is_dma:
                        s = dmasem[d.sem]
                        waits[s] = max(waits.get(s, 0), d.semval)
                    else:
                        s = engsem[d.eng]
                        waits[s] = max(waits.get(s, 0), d.ticket)
                if o in first_of:
                    for k, v in reg_tot[first_of[o] - 1].items():
                        if v:
                            s = dmasem[k]
                            waits[s] = max(waits.get(s, 0), v)
                for s, v in waits.items():
                    if known.get(s, 0) >= v:
                        continue
                    e.wait_ge(s, v)
                    known[s] = v
                if o.fn is None:
                    continue
                ins = o.fn(e)
                if isinstance(ins, (list, tuple)):
                    ins = ins[-1]
                if o.is_dma:
                    ins.then_inc(dmasem[o.sem], 16)
                elif o.signal:
                    ins.then_inc(engsem[name], 1)
            if name == "sp":
                for k, v in reg_tot[-1].items():
                    if v and known.get(dmasem[k], 0) < v:
                        e.wait_ge(dmasem[k], v)

        @block.tensor
        def _(e):
            run(e, "pe")

        @block.scalar
        def _(e):
            run(e, "act")

        @block.vector
        def _(e):
            run(e, "dve")

        @block.gpsimd
        def _(e):
            run(e, "pool")

        @block.sync
        def _(e):
            run(e, "sp")


class Arena:
    def __init__(self, ap_f32, nwords):
        self.t = ap_f32
        self.n = nwords
        self.off = 0

    def alloc(self, dtype, shape):
        n = 1
        for s in shape:
            n *= s
        words = n if dtype == F32 else (n + 1) // 2
        assert self.off + words <= self.n, f"arena overflow {self.off}+{words}>{self.n}"
        v = self.t[:, self.off:self.off + words]
        self.off += words
        if dtype == BF16:
            v = v.bitcast(BF16)
            if n % 2:
                v = v[:, 0:n]
        if len(shape) == 2:
            v = v.rearrange("p (a b) -> p a b", a=shape[0])
        elif len(shape) == 3:
            v = v.rearrange("p (a b c) -> p a b c", a=shape[0], b=shape[1])
        return v


def build_nc(T=SEQ, dbg=False, verbose=False):
    NB = T // 128
    NU = NB // 4
    assert NB % 4 == 0 and NU >= 1
    NI = T // 256

    nc = bass.Bass("TRN2", target_bir_lowering=False)

    def din(name, shape):
        return nc.dram_tensor(name, shape, F32, kind="ExternalInput").ap()

    x = din("x", [T, D])
    w_in = din("w_in", [D, INW])
    sink = din("sink", [1, NH])
    gmlp_ln_g = din("gmlp_ln_g", [1, 512])
    gmlp_ln_b = din("gmlp_ln_b", [1, 512])
    w_spatial = din("w_spatial", [8, 128, 128])
    b_spatial = din("b_spatial", [8, 128])
    attn_norm_g = din("attn_norm_g", [1, 512])
    gmlp_norm_g = din("gmlp_norm_g", [1, 512])
    w_out = din("w_out", [D, D])
    ln1_g = din("ln1_g", [1, D])
    ln1_b = din("ln1_b", [1, D])
    w_ff1 = din("w_ff1", [D, DFF])
    w_ff2 = din("w_ff2", [DFF, D])
    ln2_g = din("ln2_g", [1, D])
    ln2_b = din("ln2_b", [1, D])
    out = nc.dram_tensor("out", [T, D], F32, kind="ExternalOutput").ap()
    if dbg:
        x1s = nc.dram_tensor("x1s", [T, D], F32, kind="ExternalOutput").ap()
    else:
        x1s = nc.dram_tensor("x1s", [T, D], F32).ap()

    R = Rec()
    es = ExitStack()
    AW = 53200
    big = es.enter_context(nc.sbuf_tensor("arena", [128, AW], F32))
    psum = es.enter_context(nc.psum_tensor("psum", [128, 4096], F32))
    A = Arena(big, AW)

    def bank(b):
        return psum[:, b * 512:(b + 1) * 512]

    def small(name, n):
        return Buf(A.alloc(F32, [n]), name)

    def smalls(name, n, k=2):
        return [small(f"{name}{i}", n) for i in range(k)]

    ident = Buf(A.alloc(BF16, [128]), "ident")
    epsA = small("epsA", 1)
    epsB = small("epsB", 1)
    persist_off = A.off

    W1OFF = AW - 16384
    A.off = W1OFF
    w_in_a = Buf(A.alloc(BF16, [8, 768]), "w_in_a")
    w_in_b = Buf(A.alloc(BF16, [8, 1024]), "w_in_b")
    w_in_parts = [w_in_a, w_in_b]
    A.off = persist_off
    wq_perm = Buf(A.alloc(BF16, [8, 512]), "wq_perm")
    w_out_all = A.alloc(BF16, [8, D])
    w_out_bf = [Buf(w_out_all[:, c, :], f"w_out_bf{c}") for c in range(8)]
    wsT = Buf(A.alloc(BF16, [8, 128]), "wsT")
    biasT2 = Buf(A.alloc(BF16, [4, 768]), "biasT2")
    LG = Buf(A.alloc(F32, [512]), "LG")
    LB = Buf(A.alloc(F32, [512]), "LB")
    G1 = Buf(A.alloc(F32, [D]), "G1")
    B1 = Buf(A.alloc(F32, [D]), "B1")
    esink = small("esink", NH)
    gcol = small("gcol", 8)
    bsT = small("bsT", 8)

    xb = [Buf(A.alloc(BF16, [D]), f"xb{i}") for i in range(2)]
    xres = [Buf(A.alloc(F32, [D]), f"xres{i}") for i in range(2)]
    xT_all = A.alloc(BF16, [8, 512])
    xTq = [Buf(xT_all[:, :, b * 128:(b + 1) * 128], f"xTq{b}") for b in range(4)]
    qz = [Buf(A.alloc(BF16, [4, 4, 256]), f"qz{i}") for i in range(2)]
    kT = [Buf(A.alloc(BF16, [512]), f"kT{i}") for i in range(2)]
    vaug_all = A.alloc(BF16, [8, 2, 66])
    vaug = [Buf(vaug_all[:, s_, :, 0:65], f"vaug{s_}") for s_ in range(8)]
    gu_sb = [Buf(A.alloc(F32, [512]), f"gu_sb{i}") for i in range(4)]
    gv_sb = [Buf(A.alloc(F32, [512]), f"gv_sb{i}") for i in range(4)]
    vn = [Buf(A.alloc(BF16, [512]), f"vn{i}") for i in range(2)]
    sgu_f = [Buf(A.alloc(F32, [512]), f"sgu_f{i}") for i in range(2)]
    junk1 = Buf(A.alloc(BF16, [512]), "junk")
    junk = [junk1, junk1]
    NSG = 6
    sgu_n = [Buf(A.alloc(BF16, [512]), f"sgu_n{i}") for i in range(NSG)]
    sS2 = [Buf(A.alloc(F32, [768]), f"sS2_{i}") for i in range(2)]
    pT2 = [Buf(A.alloc(BF16, [768]), f"pT2_{i}") for i in range(2)]
    attn_f = [Buf(A.alloc(F32, [512]), f"attn_f{i}") for i in range(2)]
    attn_n = [Buf(A.alloc(BF16, [512]), f"attn_n{i}") for i in range(2)]
    mixT = Buf(A.alloc(BF16, [8, 128]), "mixT")
    ybuf = [Buf(A.alloc(F32, [D]), f"y{i}") for i in range(3)]
    tmpA = Buf(ybuf[2].ap[:, 0:384], "tmpA", extra=[ybuf[2].lt])
    tmpB = Buf(ybuf[2].ap[:, 384:768], "tmpB", extra=[ybuf[2].lt])
    ws_bf = Buf(sS2[1].ap[:, 0:512].bitcast(BF16).rearrange("p (g s) -> p g s", g=8), "ws_bf", extra=[sS2[1].lt])
    st6 = smalls("st6", 6)
    mv = smalls("mv", 2)
    lnv = smalls("lnv", 1)
    rstd = smalls("rstd", 1)
    nmr = smalls("nmr", 1)
    ssq_s = smalls("ssq_s", 1)
    lnv_s = smalls("lnv_s", 1)
    rs_s = smalls("rs_s", 1)
    ssq_a = smalls("ssq_a", 1)
    lnv_a = smalls("lnv_a", 1)
    rs_a = smalls("rs_a", 1)
    den = smalls("den", 4, 4)
    rinv = smalls("rinv", 4, 4)
    st12 = smalls("st12", 12)
    mv1 = smalls("mv1", 2)
    lnv1 = smalls("lnv1", 1)
    rstd1 = smalls("rstd1", 1)
    nmr1 = smalls("nmr1", 1)
    phaseA_end = A.off
    assert phaseA_end <= W1OFF, (phaseA_end, W1OFF)
    w1q = []
    for q in range(4):
        ap = big[:, W1OFF + q * 4096:W1OFF + (q + 1) * 4096].bitcast(BF16).rearrange("p (c n) -> p c n", c=8)
        extra = [b_.lt for b_ in w_in_parts] if q < 2 else []
        w1q.append(Buf(ap, f"w1q{q}", extra=extra))

    def load_w1q(q):
        lq = (q + 2) % 4
        R.dma("pool", lambda e: e.dma_start(out=w1q[q].ap, in_=w_ff1[:, lq * 1024:(lq + 1) * 1024].rearrange(
            "(c p) n -> p c n", p=128)), [], [w1q[q]], f"w1q{q}", 4194304)

    psT_ap = bank(0).bitcast(BF16).rearrange("p (c t) -> p c t", c=8)
    BL = [LT(f"bank{b}") for b in range(8)]
    psT = Buf(psT_ap, "psT", extra=[BL[0]])
    psM = Buf(bank(0), "psM", extra=[BL[0]])
    psQ = [Buf(bank(1), "psQ0", extra=[BL[1]]), Buf(bank(2), "psQ1", extra=[BL[2]])]
    psP = [psQ[0], psQ[0]]
    psOlo = Buf(bank(2)[:, 0:260], "psOlo", extra=[BL[2]])
    psOhi = Buf(bank(3)[:, 0:260], "psOhi", extra=[BL[3]])
    Sb = [Buf(psum[:, 2048:2816], "S0", extra=[BL[4], BL[5]]), Buf(psum[:, 3072:3840], "S1", extra=[BL[6], BL[7]])]
    psG = [[Buf(bank(3), "psG0", extra=[BL[3]]), Buf(bank(4), "psG1", extra=[BL[4]])],
           [Buf(bank(5), "psG2", extra=[BL[5]]), Buf(bank(6), "psG3", extra=[BL[6]])]]
    psV = Buf(bank(7)[:, 384:512], "psV", extra=[BL[7]])

    slopes = [2.0 ** (-(h + 1)) for h in range(NH)]

    R.dve(lambda e: e.memset(epsA.ap, EPS), [], [epsA], 1)
    R.dve(lambda e: e.memset(epsB.ap, EPS / (ALPHA * ALPHA)), [], [epsB], 1)
    R.dve(lambda e: e.memset(vaug_all[:, :, :, 64:66], 1.0), [], vaug, 32)

    idf = tmpA.ap[:, 0:128]
    idn = tmpB.ap[:, 0:128]
    R.pool(lambda e: e.iota(idf, [[1, 128]], base=0, channel_multiplier=-1, allow_small_or_imprecise_dtypes=True),
           [], [tmpA], 128)
    R.dve(lambda e: e.tensor_scalar(out=idn, in0=idf, scalar1=-1.0, scalar2=None, op0=ALU.mult), [tmpA], [tmpB], 128)
    R.dve(lambda e: e.tensor_tensor(out=idf, in0=idf, in1=idn, op=ALU.max), [tmpA, tmpB], [tmpA], 128)
    R.dve(lambda e: e.tensor_scalar(out=idf, in0=idf, scalar1=1.0, scalar2=None, op0=ALU.min), [tmpA], [tmpA], 128)
    R.dve(lambda e: e.tensor_scalar(out=ident.ap, in0=idf, scalar1=-1.0, scalar2=1.0, op0=ALU.mult, op1=ALU.add),
          [tmpA], [ident], 128)

    dd = tmpA.ap
    R.pool(lambda e: e.iota(dd.rearrange("p (j q) -> p j q", j=3), [[128, 3], [-1, 128]], base=-128,
                            channel_multiplier=1, allow_small_or_imprecise_dtypes=True), [tmpA], [tmpA], 384)
    R.dve(lambda e: e.tensor_scalar(out=tmpB.ap, in0=dd, scalar1=-1.0, scalar2=None, op0=ALU.mult), [tmpA], [tmpB], 384)
    R.dve(lambda e: e.tensor_tensor(out=dd, in0=dd, in1=tmpB.ap, op=ALU.max), [tmpA, tmpB], [tmpA], 384)
    R.dve(lambda e: e.tensor_scalar(out=tmpB.ap, in0=dd, scalar1=-128.0, scalar2=0.0, op0=ALU.add, op1=ALU.max),
          [tmpA], [tmpB], 384)
    R.dve(lambda e: e.tensor_scalar(out=tmpB.ap, in0=tmpB.ap, scalar1=1.0, scalar2=NEG, op0=ALU.min, op1=ALU.mult),
          [tmpB], [tmpB], 384)
    b2v = biasT2.ap.rearrange("p i (j h q) -> p i j h q", j=3, h=2)
    dd3 = dd.rearrange("p (j q) -> p j q", j=3)
    mk3 = tmpB.ap.rearrange("p (j q) -> p j q", j=3)
    for ic in range(4):
        for h2 in range(2):
            R.dve(lambda e, ic=ic, h2=h2: e.scalar_tensor_tensor(out=b2v[:, ic, :, h2, :], in0=dd3, scalar=-slopes[ic + 4 * h2],
                                                                 in1=mk3, op0=ALU.mult, op1=ALU.add),
                  [tmpA, tmpB], [biasT2], 384)
    for i in range(2):
        R.pool(lambda e, i=i: e.iota(qz[i].ap.rearrange("p a b c -> p (a b c)"), [[0, 4096]], base=0, channel_multiplier=0,
                                     allow_small_or_imprecise_dtypes=True), [], [qz[i]], 2048)

    R.cur_boost = 2e8
    R.dma("pool", lambda e: e.dma_start(out=w_in_a.ap, in_=w_in[:, 0:768].rearrange("(c p) n -> p c n", p=128)),
          [], [w_in_a], "w_in_a", 128 * 8 * 768 * 4)
    R.cur_boost = 0.0
    R.dma("pool", lambda e: e.dma_start(out=w_in_b.ap, in_=w_in[:, 768:1792].rearrange("(c p) n -> p c n", p=128)),
          [], [w_in_b], "w_in_b", 128 * 8 * 1024 * 4)
    for ic in range(4):
        for h2 in range(2):
            hh = ic + 4 * h2
            R.dve(lambda e, ic=ic, h2=h2, hh=hh: e.tensor_copy(
                out=wq_perm.ap[:, :, ic * 128 + h2 * 64:ic * 128 + h2 * 64 + 64],
                in_=w_in_a.ap[:, :, hh * 64:hh * 64 + 64]), [w_in_a], [wq_perm], 512)

    SETUP_LAT = [45000.0]

    def small_dma(key, dst, src, nb=4096, slow=False):
        if slow:
            R.dma("sp", lambda e: e.dma_start(out=dst_ap(dst), in_=src, allow_slow_non_contiguous=True), [], [dst[0]], key, nb,
                  lat=SETUP_LAT[0])
        else:
            R.dma("sp", lambda e: e.dma_start(out=dst_ap(dst), in_=src), [], [dst[0]], key, nb, lat=SETUP_LAT[0])

    def dst_ap(d):
        return d[1]

    small_dma("m_gc0", (gcol, gcol.ap[:, 0:4]), attn_norm_g.rearrange("o (c p) -> p (o c)", p=128), slow=True)
    small_dma("m_gc1", (gcol, gcol.ap[:, 4:8]), gmlp_norm_g.rearrange("o (c p) -> p (o c)", p=128), slow=True)
    small_dma("m_bs", (bsT, bsT.ap), b_spatial.rearrange("g t -> t g"), slow=True)
    small_dma("m_lg", (LG, LG.ap), gmlp_ln_g.partition_broadcast(128), 262144)
    small_dma("m_lb", (LB, LB.ap), gmlp_ln_b.partition_broadcast(128), 262144)
    small_dma("m_g1", (G1, G1.ap), ln1_g.partition_broadcast(128), 524288)
    small_dma("m_b1", (B1, B1.ap), ln1_b.partition_broadcast(128), 524288)
    small_dma("m_sk", (esink, esink.ap), sink.partition_broadcast(128))
    R.act(lambda e: e.activation(out=esink.ap, in_=esink.ap, func=AF.Exp), [esink], [esink], 8, "el")
    for c in range(8):
        st = ybuf[c % 2]
        R.dma("sp", lambda e, c=c, st=st: e.dma_start(out=st.ap, in_=w_out[c * 128:(c + 1) * 128, :]),
              [], [st], f"stage{c % 2}", 524288, lat=(55000.0 if c < 2 else 5000.0))
        R.dve(lambda e, c=c, st=st: e.tensor_scalar(out=w_out_bf[c].ap, in0=st.ap, scalar1=gcol.ap[:, c:c + 1],
                                                     scalar2=None, op0=ALU.mult), [st, gcol], [w_out_bf[c]], 1024)
    wsf = ybuf[2]
    rowsum = small("rowsum", 8)
    R.dma("sp", lambda e: e.dma_start(out=wsf.ap.rearrange("p (g s) -> p g s", g=8), in_=w_spatial.rearrange("g t s -> t g s")),
          [], [wsf], "m_wsf", 524288, lat=60000.0)
    def fold_lb():
        R.dve(lambda e: e.tensor_reduce(out=rowsum.ap, in_=wsf.ap.rearrange("p (g s) -> p g s", g=8), axis=mybir.AxisListType.X,
                                        op=ALU.add), [wsf, wq_perm, qz[0]], [rowsum], 1024)
        for g in range(8):
            R.dve(lambda e, g=g: e.tensor_scalar(out=LB.ap[:, g * 64:(g + 1) * 64], in0=LB.ap[:, g * 64:(g + 1) * 64],
                                                 scalar1=rowsum.ap[:, g:g + 1], scalar2=bsT.ap[:, g:g + 1],
                                                 op0=ALU.mult, op1=ALU.add), [LB, rowsum, bsT], [LB], 64)

    R.dma("pool", lambda e: e.dma_start(out=ws_bf.ap, in_=w_spatial.rearrange("g t s -> t g s")),
          [], [ws_bf], "ws", 524288)
    R.pe(lambda e: [e.transpose(psT.ap[:, g, :], ws_bf.ap[:, g, :], ident.ap) for g in range(8)],
         [ws_bf, ident], [psT], [128] * 8)
    R.dve(lambda e: e.tensor_copy(out=wsT.ap, in_=psT.ap), [psT], [wsT], 1024)

    def rsqrt_act(src_ap, src_buf, eps_buf, scale, lnb, outb):
        R.act(lambda e: e.activation(out=lnb.ap, in_=src_ap, func=AF.Ln, bias=eps_buf.ap, scale=scale),
              [src_buf, eps_buf], [lnb], 1, "el")
        R.act(lambda e: e.activation(out=outb.ap, in_=lnb.ap, func=AF.Exp, scale=-0.5), [lnb], [outb], 1, "el")

    def front(u):
        for b in range(4):
            blk = 4 * u + b
            xbb = xb[blk % 2]
            R.dma("pool", lambda e, blk=blk, xbb=xbb: e.dma_start(out=xbb.ap, in_=x[blk * 128:(blk + 1) * 128, :]),
                  [], [xbb], f"xb{blk % 2}", 524288)
            R.pe(lambda e, xbb=xbb: [e.transpose(psT.ap[:, c, :], xbb.ap[:, c * 128:(c + 1) * 128], ident.ap)
                                     for c in range(8)], [xbb, ident], [psT], [128] * 8)
            R.dve(lambda e, b=b: e.tensor_copy(out=xTq[b].ap, in_=psT.ap), [psT], [xTq[b]], 1024)
        for grp in range(5):
            pq = psQ[grp % 2]
            if grp < 4:
                wsrc, rd, c0 = wq_perm.ap, [wq_perm], grp * 128
            else:
                wsrc, rd, c0 = w_in_a.ap, [w_in_a], 512
            R.pe(lambda e, wsrc=wsrc, c0=c0, pq=pq: [
                e.matmul(pq.ap, lhsT=wsrc[:, c, c0:c0 + 128], rhs=xT_all[:, c, :], start=(c == 0), stop=(c == 7))
                for c in range(8)], rd + xTq, [pq], [512] * 8)
            if grp < 4:
                dst = qz[u % 2]
                R.dve(lambda e, grp=grp, pq=pq, dst=dst: e.tensor_copy(
                    out=dst.ap[0:64, grp, :, 0:128], in_=pq.ap[0:64, :].rearrange("p (b q) -> p b q", b=4)), [pq], [dst], 512)
                R.act(lambda e, grp=grp, pq=pq, dst=dst: e.activation(
                    out=dst.ap[64:128, grp, :, 128:256], in_=pq.ap[64:128, :].rearrange("p (b q) -> p b q", b=4),
                    func=AF.Copy), [pq], [dst], 512)
            else:
                dst = kT[u % 2]
                R.act(lambda e, pq=pq, dst=dst: e.activation(out=dst.ap, in_=pq.ap, func=AF.Copy), [pq], [dst], 512)
        gel = []
        for b in range(4):
            blk = 4 * u + b
            xs = xTq[b]
            R.pe(lambda e, xs=xs: [e.matmul(psV.ap, lhsT=xs.ap[:, c, :], rhs=w_in_a.ap[:, c, 640:768],
                                            start=(c == 0), stop=(c == 7)) for c in range(8)],
                 [xs, w_in_a], [psV], [128] * 8)
            va = vaug[blk % 8]
            R.dve(lambda e, va=va: e.tensor_copy(out=va.ap[:, :, 0:64], in_=psV.ap.rearrange("p (g d) -> p g d", g=2)),
                  [psV], [va], 128)
            pgu, pgv = psG[b % 2]
            R.pe(lambda e, xs=xs, pgu=pgu: [e.matmul(pgu.ap, lhsT=xs.ap[:, c, :], rhs=w_in_b.ap[:, c, 0:512],
                                                     start=(c == 0), stop=(c == 7)) for c in range(8)],
                 [xs, w_in_b], [pgu], [512] * 8)
            gel.append((gu_sb[b], pgu))
            R.pe(lambda e, xs=xs, pgv=pgv: [e.matmul(pgv.ap, lhsT=xs.ap[:, c, :], rhs=w_in_b.ap[:, c, 512:1024],
                                                     start=(c == 0), stop=(c == 7)) for c in range(8)],
                 [xs, w_in_b], [pgv], [512] * 8)
            gel.append((gv_sb[b], pgv))
            if b % 2 == 1:
                srcs = [g_[1] for g_ in gel]
                for (dst_, src_) in gel:
                    R.act(lambda e, dst_=dst_, src_=src_: e.activation(out=dst_.ap, in_=src_.ap, func=AF.Gelu),
                          srcs, [dst_], 512, "gelu")
                gel = []

    def sgu_vec(b, blk):
        p = blk % 2
        g = gv_sb[b]
        R.dve(lambda e: e.bn_stats(out=st6[p].ap, in_=g.ap), [g], [st6[p]], 512)
        R.dve(lambda e: e.bn_aggr(out=mv[p].ap, in_=st6[p].ap), [st6[p]], [mv[p]], 26)
        rsqrt_act(mv[p].ap[:, 1:2], mv[p], epsA, 1.0, lnv[p], rstd[p])
        R.dve(lambda e: e.scalar_tensor_tensor(out=nmr[p].ap, in0=mv[p].ap[:, 0:1], scalar=-1.0, in1=rstd[p].ap,
                                               op0=ALU.mult, op1=ALU.mult), [mv[p], rstd[p]], [nmr[p]], 1)
        R.act(lambda e: e.activation(out=vn[p].ap, in_=g.ap, func=AF.Identity, bias=nmr[p].ap, scale=rstd[p].ap),
              [g, nmr[p], rstd[p]], [vn[p]], 512)

    def sgu_mm(b, blk):
        p = blk % 2
        R.pe(lambda e: [e.matmul(psM.ap[:, g * 64:(g + 1) * 64], lhsT=wsT.ap[:, g, :], rhs=vn[p].ap[:, g * 64:(g + 1) * 64],
                                 start=True, stop=True) for g in range(8)], [wsT, vn[p]], [psM], [128] * 8)
        R.dve(lambda e: e.tensor_tensor(out=sgu_f[p].ap, in0=psM.ap, in1=LG.ap, op=ALU.mult), [psM, LG], [sgu_f[p]], 512)
        R.dve(lambda e: e.tensor_tensor(out=sgu_f[p].ap, in0=sgu_f[p].ap, in1=LB.ap, op=ALU.add), [sgu_f[p], LB], [sgu_f[p]], 512)
        R.dve(lambda e: e.tensor_tensor(out=sgu_f[p].ap, in0=sgu_f[p].ap, in1=gu_sb[b].ap, op=ALU.mult),
              [sgu_f[p], gu_sb[b]], [sgu_f[p]], 512)
        R.act(lambda e: e.activation(out=junk[0].ap, in_=sgu_f[p].ap, func=AF.Square, accum_out=ssq_s[p].ap),
              [sgu_f[p]], [junk[0], ssq_s[p]], 512)
        rsqrt_act(ssq_s[p].ap, ssq_s[p], epsA, 1.0 / 512.0, lnv_s[p], rs_s[p])
        dst = sgu_n[blk % NSG]
        R.act(lambda e: e.activation(out=dst.ap, in_=sgu_f[p].ap, func=AF.Identity, scale=rs_s[p].ap),
              [sgu_f[p], rs_s[p]], [dst], 512)

    def attention(m):
        p = m % 2
        js = [j for j in range(3) if 0 <= m - 1 + j < NB]
        lo, hi = js[0] * 256, (js[-1] + 1) * 256
        qs = qz[(m // 4) % 2]
        qb = m % 4
        af = attn_f[p]
        for ic in range(4):
            pS = Sb[ic % 2]
            sSb = sS2[ic % 2]
            pTb = pT2[ic % 2]
            kks = [(j, kT[((m - 1 + j) // 4) % 2], ((m - 1 + j) % 4) * 128) for j in js]
            R.pe(lambda e, kks=kks, ic=ic, pS=pS: [
                e.matmul(pS.ap[:, j * 256:(j + 1) * 256], lhsT=ks.ap[:, koff:koff + 128],
                         rhs=qs.ap[:, ic, qb, :], start=True, stop=True)
                for (j, ks, koff) in kks], [k[1] for k in kks] + [qs], [pS], [256] * len(js))
            R.dve(lambda e, ic=ic, pS=pS, sSb=sSb: e.scalar_tensor_tensor(
                out=sSb.ap[:, lo:hi], in0=pS.ap[:, lo:hi], scalar=0.125, in1=biasT2.ap[:, ic, lo:hi],
                op0=ALU.mult, op1=ALU.add), [pS, biasT2], [sSb], hi - lo)
            R.act(lambda e, sSb=sSb, pTb=pTb: e.activation(out=pTb.ap[:, lo:hi], in_=sSb.ap[:, lo:hi], func=AF.Exp),
                  [sSb], [pTb], hi - lo, "el")
            vas = [(j, vaug[(m - 1 + j) % 8]) for j in js]
            for h2 in range(2):
                pO = psOlo if h2 == 0 else psOhi
                R.pe(lambda e, vas=vas, h2=h2, ic=ic, pO=pO, pTb=pTb: [
                    e.matmul(pO.ap[:, ic * 65:ic * 65 + 65], lhsT=pTb.ap[:, j * 256 + h2 * 128:j * 256 + h2 * 128 + 128],
                             rhs=va.ap[:, h2, :], start=(j == js[0]), stop=(j == js[-1])) for (j, va) in vas],
                     [pTb] + [v[1] for v in vas], [pO], [128] * len(js))
        for half in range(2):
            pO = psOlo if half == 0 else psOhi
            o3 = pO.ap.rearrange("p (h d) -> p h d", h=4)
            dn, ri = den[2 * p + half], rinv[2 * p + half]
            R.dve(lambda e, o3=o3, dn=dn, half=half: e.tensor_tensor(
                out=dn.ap, in0=o3[:, :, 64], in1=esink.ap[:, half * 4:(half + 1) * 4], op=ALU.add),
                  [pO, esink], [dn], 4)
            R.dve(lambda e, dn=dn, ri=ri: e.reciprocal(out=ri.ap, in_=dn.ap), [dn], [ri], 4)
            R.dve(lambda e, o3=o3, ri=ri, half=half: e.tensor_tensor(
                out=af.ap[:, half * 256:(half + 1) * 256].rearrange("p (h d) -> p h d", h=4),
                in0=o3[:, :, 0:64], in1=ri.ap.unsqueeze(2).to_broadcast([128, 4, 64]), op=ALU.mult),
                  [pO, ri], [af], 256)
        R.act(lambda e: e.activation(out=junk[1].ap, in_=af.ap, func=AF.Square, accum_out=ssq_a[p].ap),
              [af], [junk[1], ssq_a[p]], 512)
        rsqrt_act(ssq_a[p].ap, ssq_a[p], epsA, 1.0 / 512.0, lnv_a[p], rs_a[p])
        R.act(lambda e: e.activation(out=attn_n[p].ap, in_=af.ap, func=AF.Identity, scale=rs_a[p].ap),
              [af, rs_a[p]], [attn_n[p]], 512)

    def mix(m):
        p = m % 2
        p3 = m % 3
        xr = xres[p]
        R.dma("sp", lambda e: e.dma_start(out=xr.ap, in_=x[m * 128:(m + 1) * 128, :]), [], [xr], f"xres{p}", 524288)
        sg = sgu_n[m % NSG]
        an = attn_n[p]
        R.pe(lambda e: [e.transpose(psT.ap[:, c, :], (an.ap[:, c * 128:(c + 1) * 128] if c < 4 else
                                                      sg.ap[:, (c - 4) * 128:(c - 3) * 128]), ident.ap) for c in range(8)],
             [an, sg, ident], [psT], [128] * 8)
        R.dve(lambda e: e.tensor_copy(out=mixT.ap, in_=psT.ap), [psT], [mixT], 1024)
        yb = ybuf[p3]
        for hf in range(2):
            pp = psP[hf]
            R.pe(lambda e, hf=hf, pp=pp: [e.matmul(pp.ap, lhsT=mixT.ap[:, c, :], rhs=w_out_bf[c].ap[:, hf * 512:(hf + 1) * 512],
                                                   start=(c == 0), stop=(c == 7)) for c in range(8)],
                 [mixT] + w_out_bf, [pp], [512] * 8)
            R.dve(lambda e, hf=hf, pp=pp: e.scalar_tensor_tensor(
                out=yb.ap[:, hf * 512:(hf + 1) * 512], in0=pp.ap, scalar=1.0 / ALPHA,
                in1=xr.ap[:, hf * 512:(hf + 1) * 512], op0=ALU.mult, op1=ALU.add), [pp, xr], [yb], 512)
        for hf in range(2):
            R.dve(lambda e, hf=hf: e.bn_stats(out=st12[p].ap[:, hf * 6:(hf + 1) * 6], in_=yb.ap[:, hf * 512:(hf + 1) * 512]),
                  [yb], [st12[p]], 512)
        R.dve(lambda e: e.bn_aggr(out=mv1[p].ap, in_=st12[p].ap), [st12[p]], [mv1[p]], 52)
        rsqrt_act(mv1[p].ap[:, 1:2], mv1[p], epsB, 1.0, lnv1[p], rstd1[p])
        R.dve(lambda e: e.scalar_tensor_tensor(out=nmr1[p].ap, in0=mv1[p].ap[:, 0:1], scalar=-1.0, in1=rstd1[p].ap,
                                               op0=ALU.mult, op1=ALU.mult), [mv1[p], rstd1[p]], [nmr1[p]], 1)
        R.act(lambda e: e.activation(out=yb.ap, in_=yb.ap, func=AF.Identity, bias=nmr1[p].ap, scale=rstd1[p].ap),
              [yb, nmr1[p], rstd1[p]], [yb], 1024)
        for (cst, aop) in ((G1, ALU.mult), (B1, ALU.add)):
            for qq in range(4):
                R.pool(lambda e, cst=cst, aop=aop, qq=qq: e.tensor_tensor(
                    out=yb.ap[:, qq * 256:(qq + 1) * 256], in0=yb.ap[:, qq * 256:(qq + 1) * 256],
                    in1=cst.ap[:, qq * 256:(qq + 1) * 256], op=aop), [yb, cst], [yb], 256)
        R.dma("sp", lambda e: e.dma_start(out=x1s[m * 128:(m + 1) * 128, :], in_=yb.ap), [yb], [], f"x1t{p3}", 524288)

    for u in range(NU + 1):
        if u < NU:
            R.cur_boost = 1e8 if u == 0 else 0.0
            front(u)
            R.cur_boost = 0.0
        if u == 0:
            fold_lb()
        if u == max(NU - 2, 0):
            load_w1q(2)
            load_w1q(3)
        if u == NU - 1:
            load_w1q(0)
            load_w1q(1)
        for i in range(4):
            blk = 4 * u + i
            m = 4 * u - 1 + i
            do_sgu = u < NU
            do_att = (0 <= m < NB) and (u < NU or i == 0)
            if do_sgu:
                sgu_vec(i, blk)
            if do_att:
                attention(m)
            if do_sgu:
                sgu_mm(i, blk)
            if do_att:
                mix(m)

    R.barrier()
    A.off = persist_off
    w2_all = A.alloc(BF16, [32, D])
    w2q = [Buf(w2_all[:, q * 8:(q + 1) * 8, :], f"w2q{q}") for q in range(4)]
    G2 = Buf(A.alloc(F32, [D]), "G2")
    B2 = Buf(A.alloc(F32, [D]), "B2")
    hT_all = A.alloc(BF16, [32, 256])
    hT = [Buf(hT_all[:, fc, :], f"hT{fc}") for fc in range(32)]
    x1 = [Buf(A.alloc(F32, [2, D]), f"x1_{i}") for i in range(2)]
    x1b = [Buf(A.alloc(BF16, [2, D]), f"x1b{i}") for i in range(2)]
    x1T_all = [A.alloc(BF16, [8, 256]) for i in range(2)]
    x1T = [[Buf(x1T_all[i][:, :, s_ * 128:(s_ + 1) * 128], f"x1T{i}_{s_}") for s_ in range(2)] for i in range(2)]
    rbuf = [Buf(A.alloc(F32, [256]), f"rbuf{i}") for i in range(3)]
    y2 = [Buf(A.alloc(F32, [2, D]), f"y2_{i}") for i in range(2)]
    stB = smalls("stB", 12)
    mvB = smalls("mvB", 2)
    lnvB = smalls("lnvB", 1)
    rstdB = smalls("rstdB", 1)
    nmrB = smalls("nmrB", 1)
    phaseB_end = A.off
    assert phaseB_end <= W1OFF, (phaseB_end, W1OFF)
    if verbose:
        print("arena words: phaseA", phaseA_end, "phaseB", phaseB_end, "of", AW)

    psTB = Buf(psT_ap, "psTB", extra=[BL[0]])
    psH = [Buf(bank(1 + i)[:, 0:256], f"psH{i}", extra=[BL[1 + i]]) for i in range(3)]
    psY = [[Buf(bank(4 + 2 * s_ + hf), f"psY{s_}{hf}", extra=[BL[4 + 2 * s_ + hf]]) for hf in range(2)] for s_ in range(2)]

    def load_w2():
        for q in range(4):
            R.dma("pool", lambda e, q=q: e.dma_start(out=w2q[q].ap, in_=w_ff2[q * 1024:(q + 1) * 1024, :].rearrange(
                "(c p) n -> p c n", p=128)), [x1b[0]], [w2q[q]], f"w2q{q}", 4194304)
    SETUP_LAT[0] = None
    small_dma("m_g2", (G2, G2.ap), ln2_g.partition_broadcast(128), 524288)
    small_dma("m_b2", (B2, B2.ap), ln2_b.partition_broadcast(128), 524288)

    def b_load(i):
        xs = x1[i % 2]
        xbb = x1b[i % 2]
        src = x1s[i * 256:(i + 1) * 256, :].rearrange("(s p) d -> p s d", p=128)
        R.dma("sp", lambda e: e.dma_start(out=xs.ap, in_=src), [], [xs], f"x1L{i % 2}", 1048576)
        R.dma("pool", lambda e: e.dma_start(out=xbb.ap, in_=src), [], [xbb], f"x1bL{i % 2}", 1048576)

    def b_transpose(i):
        xbb = x1b[i % 2]
        for s_ in range(2):
            R.pe(lambda e, s_=s_: [e.transpose(psTB.ap[:, c, :], xbb.ap[:, s_, c * 128:(c + 1) * 128], ident.ap)
                                   for c in range(8)], [xbb, ident], [psTB], [128] * 8)
            dst = x1T[i % 2][s_]
            R.dve(lambda e, dst=dst: e.tensor_copy(out=dst.ap, in_=psTB.ap), [psTB], [dst], 1024)

    def b_ff1(i):
        xt = x1T_all[i % 2]
        for fc in range(32):
            ph = psH[fc % 3]
            rb = rbuf[fc % 3]
            R.pe(lambda e, fc=fc, ph=ph: [e.matmul(ph.ap, lhsT=w1q[(fc // 8 + 2) % 4].ap[:, c, (fc % 8) * 128:(fc % 8 + 1) * 128], rhs=xt[:, c, :],
                                                   start=(c == 0), stop=(c == 7)) for c in range(8)],
                 [w1q[(fc // 8 + 2) % 4]] + x1T[i % 2], [ph], [256] * 8)
            R.act(lambda e, ph=ph, rb=rb: e.activation(out=rb.ap, in_=ph.ap, func=AF.Relu), [ph], [rb], 256)
            R.dve(lambda e, fc=fc, rb=rb: e.tensor_tensor(out=hT[fc].ap, in0=rb.ap, in1=rb.ap, op=ALU.mult),
                  [rb], [hT[fc]], 256)

    def b_ff2(i):
        xs = x1[i % 2]
        yo = y2[i % 2]
        for s_ in range(2):
            for q in range(4):
                R.pe(lambda e, s_=s_, q=q: [e.matmul(psY[s_][hf].ap, lhsT=hT[fc].ap[:, s_ * 128:(s_ + 1) * 128],
                                                     rhs=w2_all[:, fc, hf * 512:(hf + 1) * 512],
                                                     start=(fc == 0), stop=(fc == 31))
                                            for fc in range(q * 8, q * 8 + 8) for hf in range(2)],
                     hT[q * 8:q * 8 + 8] + [w2q[q]], psY[s_], [512] * 16)
        for s_ in range(2):
            p = s_
            for hf in range(2):
                py = psY[s_][hf]
                R.dve(lambda e, s_=s_, hf=hf, py=py: e.scalar_tensor_tensor(
                    out=yo.ap[:, s_, hf * 512:(hf + 1) * 512], in0=py.ap, scalar=1.0 / ALPHA,
                    in1=xs.ap[:, s_, hf * 512:(hf + 1) * 512], op0=ALU.mult, op1=ALU.add), [py, xs], [yo], 512)
            for hf in range(2):
                R.dve(lambda e, s_=s_, hf=hf, p=p: e.bn_stats(out=stB[p].ap[:, hf * 6:(hf + 1) * 6],
                                                         in_=yo.ap[:, s_, hf * 512:(hf + 1) * 512]), [yo], [stB[p]], 512)
            R.dve(lambda e, p=p: e.bn_aggr(out=mvB[p].ap, in_=stB[p].ap), [stB[p]], [mvB[p]], 52)
            rsqrt_act(mvB[p].ap[:, 1:2], mvB[p], epsB, 1.0, lnvB[p], rstdB[p])
            R.dve(lambda e, p=p: e.scalar_tensor_tensor(out=nmrB[p].ap, in0=mvB[p].ap[:, 0:1], scalar=-1.0, in1=rstdB[p].ap,
                                                        op0=ALU.mult, op1=ALU.mult), [mvB[p], rstdB[p]], [nmrB[p]], 1)
            R.act(lambda e, s_=s_, p=p: e.activation(out=yo.ap[:, s_, :], in_=yo.ap[:, s_, :], func=AF.Identity,
                                                     bias=nmrB[p].ap, scale=rstdB[p].ap), [yo, nmrB[p], rstdB[p]], [yo], 1024)
            R.pool(lambda e, s_=s_: e.tensor_tensor(out=yo.ap[:, s_, :], in0=yo.ap[:, s_, :], in1=G2.ap, op=ALU.mult),
                   [yo, G2], [yo], 1024)
            R.pool(lambda e, s_=s_: e.tensor_tensor(out=yo.ap[:, s_, :], in0=yo.ap[:, s_, :], in1=B2.ap, op=ALU.add),
                   [yo, B2], [yo], 1024)
        R.dma("sp", lambda e: e.dma_start(out=out[i * 256:(i + 1) * 256, :].rearrange("(s p) d -> p s d", p=128), in_=yo.ap),
              [yo], [], f"out{i % 2}", 1048576)

    b_load(0)
    load_w2()
    b_transpose(0)
    for i in range(NI):
        if i + 1 < NI:
            b_load(i + 1)
        b_ff1(i)
        if i + 1 < NI:
            b_transpose(i + 1)
        b_ff2(i)

    R.emit(nc, es)
    if verbose:
        print("sim times us", [t / 1e3 for t in R.sim_times], "busy", R.sim_busy, "ops", len(R.all))
    es.close()
    return nc


_WNAMES = ["w_in", "sink", "gmlp_ln_g", "gmlp_ln_b", "w_spatial", "b_spatial", "attn_norm_g", "gmlp_norm_g",
           "w_out", "ln1_g", "ln1_b", "w_ff1", "w_ff2", "ln2_g", "ln2_b"]


def kernel(**inputs):
    x = np.ascontiguousarray(np.asarray(inputs["x"], dtype=np.float32))
    B, S, _ = x.shape
    shared = {}
    for n in _WNAMES:
        a = np.asarray(inputs[n], dtype=np.float32)
        a = a[0]
        if a.ndim == 1:
            a = a[None, :]
        shared[n] = np.ascontiguousarray(a)
    nc = build_nc(S)
    in_maps = []
    for b in range(B):
        m = {"x": x[b]}
        m.update(shared)
        in_maps.append(m)
    res = run_bass_kernel_spmd(nc, in_maps, core_ids=list(range(B)))
    return np.stack([np.asarray(r["out"], dtype=np.float32) for r in res.results], axis=0)
```

```python
import numpy as np
from contextlib import ExitStack

import concourse.bass as bass
import concourse.mybir as mybir
from concourse.bass_utils import run_bass_kernel_spmd

F32 = mybir.dt.float32
BF16 = mybir.dt.bfloat16
AF = mybir.ActivationFunctionType
ALU = mybir.AluOpType

D = 1024
DFF = 4096
INW = 1792
NH = 8
ALPHA = 2.0 ** 0.25
EPS = 1e-5
NEG = -30000.0
N_CORES = 8
SEQ = 8192

ENGS = ("pe", "act", "dve", "pool", "sp")


class LT:
    __slots__ = ("name", "w", "rs")

    def __init__(self, name):
        self.name = name
        self.w = None
        self.rs = []


class Buf:
    __slots__ = ("ap", "lt", "extra")

    def __init__(self, ap, name, extra=()):
        self.ap = ap
        self.lt = LT(name)
        self.extra = list(extra)


class Op:
    __slots__ = ("eng", "fn", "alldeps", "deps", "signal", "ticket", "is_dma", "sem", "semval", "idx", "dur",
                 "tbl", "region", "lat", "fin", "ndeps", "users", "boost")


def _d_pe(cols):
    return sum(max(c, 128) * 0.45 + 20 for c in cols)


def _d_dve(n):
    return (n + 151) / 0.96


def _d_act(n):
    return 210 + 0.95 * n


def _d_pool(n):
    return 250 + 2.2 * n


class Rec:
    def __init__(self):
        self.all = []
        self.region = 0
        self.dma_keys = {}
        self.cur_boost = 0.0

    def _lts(self, xs):
        r = []
        for x in xs:
            if isinstance(x, Buf):
                r.append(x.lt)
                r.extend(x.extra)
            else:
                r.append(x)
        return r

    def _new(self, eng, fn, reads, writes, dur, is_dma=False, tbl=None, sem=None, lat=0.0):
        o = Op()
        o.eng = eng
        o.fn = fn
        o.is_dma = is_dma
        o.signal = False
        o.ticket = None
        o.sem = sem
        o.semval = None
        o.dur = dur
        o.tbl = tbl
        o.lat = lat
        o.boost = self.cur_boost
        o.region = self.region
        o.idx = len(self.all)
        deps = set()
        reads = self._lts(reads)
        writes = self._lts(writes)
        for t in reads:
            if t.w is not None:
                deps.add(t.w)
        for t in writes:
            if t.w is not None:
                deps.add(t.w)
            deps.update(t.rs)
        deps.discard(o)
        o.alldeps = deps
        for t in reads:
            t.rs.append(o)
        for t in writes:
            t.w = o
            t.rs = []
        self.all.append(o)
        return o

    def pe(self, fns, reads, writes, cols):
        return self._new("pe", fns, reads, writes, _d_pe(cols))

    def dve(self, fn, reads, writes, n):
        return self._new("dve", fn, reads, writes, _d_dve(n))

    def act(self, fn, reads, writes, n, tbl=None):
        return self._new("act", fn, reads, writes, _d_act(n), tbl=tbl)

    def pool(self, fn, reads, writes, n):
        return self._new("pool", fn, reads, writes, _d_pool(n))

    def dma(self, queue, fn, reads, writes, semkey, nbytes, lat=None):
        assert self.dma_keys.setdefault(semkey, queue) == queue
        issue = 120.0 if queue == "sp" else 1200.0
        if lat is None:
            lat = 2000.0 + nbytes / 150.0
        return self._new(queue, fn, reads, writes, issue, is_dma=True, sem=semkey, lat=lat)

    def barrier(self):
        self.region += 1

    def schedule(self):
        import heapq
        streams = {e: [] for e in ENGS}
        nreg = self.region + 1
        bar = []
        HOP = 100.0
        TBL = 1400.0
        for reg in range(nreg):
            ops = [o for o in self.all if o.region == reg]
            for o in ops:
                o.users = []
                o.ndeps = 0
            for o in ops:
                for d in o.alldeps:
                    if d.region == reg:
                        d.users.append(o)
                        o.ndeps += 1
            tail = {}
            for o in reversed(ops):
                t = 0.0
                for u in o.users:
                    tu = tail[u]
                    if tu > t:
                        t = tu
                tail[o] = t + o.dur + o.lat + HOP + o.boost
            efree = {e: 0.0 for e in ENGS}
            tblstate = [None]
            wait_h = {e: [] for e in ENGS}
            rdy_h = {e: [] for e in ENGS}
            ready_t = {}
            for o in ops:
                if o.ndeps == 0:
                    heapq.heappush(wait_h[o.eng], (0.0, o.idx, o))
                    ready_t[o] = 0.0
            nleft = len(ops)
            local = {e: [] for e in ENGS}
            while nleft:
                best = None
                for e in ENGS:
                    wh, rh = wait_h[e], rdy_h[e]
                    while wh and wh[0][0] <= efree[e]:
                        t, i, o = heapq.heappop(wh)
                        heapq.heappush(rh, (-tail[o], i, o))
                    cand = None
                    if rh:
                        if e == "act":
                            pick = None
                            for ent in heapq.nsmallest(16, rh):
                                o = ent[2]
                                if o.tbl is None or o.tbl == tblstate[0]:
                                    pick = ent
                                    break
                            if pick is None:
                                pick = rh[0]
                                st = efree[e] + TBL
                            else:
                                st = efree[e]
                            cand = (st, pick[0], pick[1], pick[2], pick)
                        else:
                            ent = rh[0]
                            cand = (efree[e], ent[0], ent[1], ent[2], ent)
                    elif wh:
                        t, i, o = wh[0]
                        st = t
                        if e == "act" and o.tbl is not None and o.tbl != tblstate[0]:
                            st = max(t, efree[e] + TBL)
                        cand = (st, -tail[o], i, o, None)
                    if cand is not None and (best is None or cand[:3] < best[:3]):
                        best = cand + (e,)
                st, _, i, o, ent, e = best
                if ent is not None:
                    rdy_h[e].remove(ent)
                    heapq.heapify(rdy_h[e])
                else:
                    heapq.heappop(wait_h[e])
                if e == "act" and o.tbl is not None:
                    tblstate[0] = o.tbl
                efree[e] = st + o.dur
                o.fin = st + o.dur + o.lat + HOP
                local[e].append(o)
                nleft -= 1
                for u in o.users:
                    u.ndeps -= 1
                    ready_t[u] = max(ready_t.get(u, 0.0), o.fin)
                    if u.ndeps == 0:
                        heapq.heappush(wait_h[u.eng], (ready_t[u], u.idx, u))
            for e in ENGS:
                streams[e].extend(local[e])
            bar.append({e: (local[e][-1] if local[e] else None) for e in ENGS})
            self.sim_time = max(efree.values())
            self.sim_times = getattr(self, "sim_times", []) + [self.sim_time]
            self.sim_busy = getattr(self, "sim_busy", []) + [{e: round(sum(o.dur for o in local[e]) / 1e3) for e in ENGS}]
        return streams, bar

    def emit(self, nc, es):
        streams, bar = self.schedule()
        first_of = {}
        for e in ENGS:
            seen = set()
            for o in streams[e]:
                if o.region > 0 and (e, o.region) not in seen:
                    seen.add((e, o.region))
                    first_of[o] = o.region
        for e in ENGS:
            for o in streams[e]:
                deps = set()
                for d in o.alldeps:
                    if d.is_dma:
                        deps.add(d)
                    elif d.eng == "pe" and o.eng == "pe":
                        continue
                    else:
                        d.signal = True
                        deps.add(d)
                o.deps = deps
        for o, reg in first_of.items():
            for r in range(reg):
                for e2, lo in bar[r].items():
                    if lo is not None and not lo.is_dma and not (lo.eng == "pe" and o.eng == "pe"):
                        lo.signal = True
                        o.deps.add(lo)
        engsem = {e: es.enter_context(nc.semaphore("c_" + e)) for e in ENGS if e != "sp"}
        dmasem = {k: es.enter_context(nc.semaphore("d_" + k)) for k in self.dma_keys}
        dcount = {k: 0 for k in self.dma_keys}
        region_dma_tot = {}
        for e in ENGS:
            k = 0
            for o in streams[e]:
                if o.is_dma:
                    dcount[o.sem] += 16
                    o.semval = dcount[o.sem]
                elif o.signal:
                    k += 1
                    o.ticket = k
        tot = {k: 0 for k in self.dma_keys}
        reg_tot = []
        for r in range(self.region + 1):
            for o in self.all:
                if o.region == r and o.is_dma:
                    tot[o.sem] += 16
            for k_ in tot:
                if k_.startswith("w1q"):
                    tot[k_] = 0
            reg_tot.append(dict(tot))
        block = es.enter_context(nc.Block())

        def run(e, name):
            known = {}
            for o in streams[name]:
                waits = {}
                for d in o.deps:
                    if d.is_dma:
                        s = dmasem[d.sem]
                        waits[s] = max(waits.get(s, 0), d.semval)
                    else:
                        s = engsem[d.eng]
                        waits[s] = max(waits.get(s, 0), d.ticket)
                if o in first_of:
                    for k, v in reg_tot[first_of[o] - 1].items():
                        if v:
                            s = dmasem[k]
                            waits[s] = max(waits.get(s, 0), v)
                for s, v in waits.items():
                    if known.get(s, 0) >= v:
                        continue
                    e.wait_ge(s, v)
                    known[s] = v
                if o.fn is None:
                    continue
                ins = o.fn(e)
                if isinstance(ins, (list, tuple)):
                    ins = ins[-1]
                if o.is_dma:
                    ins.then_inc(dmasem[o.sem], 16)
                elif o.signal:
                    ins.then_inc(engsem[name], 1)
            if name == "sp":
                for k, v in reg_tot[-1].items():
                    if v and known.get(dmasem[k], 0) < v:
                        e.wait_ge(dmasem[k], v)

        @block.tensor
        def _(e):
            run(e, "pe")

        @block.scalar
        def _(e):
            run(e, "act")

        @block.vector
        def _(e):
            run(e, "dve")

        @block.gpsimd
        def _(e):
            run(e, "pool")

        @block.sync
        def _(e):
            run(e, "sp")


class Arena:
    def __init__(self, ap_f32, nwords):
        self.t = ap_f32
        self.n = nwords
        self.off = 0

    def alloc(self, dtype, shape):
        n = 1
        for s in shape:
            n *= s
        words = n if dtype == F32 else (n + 1) // 2
        assert self.off + words <= self.n, f"arena overflow {self.off}+{words}>{self.n}"
        v = self.t[:, self.off:self.off + words]
        self.off += words
        if dtype == BF16:
            v = v.bitcast(BF16)
            if n % 2:
                v = v[:, 0:n]
        if len(shape) == 2:
            v = v.rearrange("p (a b) -> p a b", a=shape[0])
        elif len(shape) == 3:
            v = v.rearrange("p (a b c) -> p a b c", a=shape[0], b=shape[1])
        return v


def build_nc(T=SEQ, dbg=False, verbose=False):
    NB = T // 128
    NU = NB // 4
    assert NB % 4 == 0 and NU >= 1
    NI = T // 256

    nc = bass.Bass("TRN2", target_bir_lowering=False)

    def din(name, shape):
        return nc.dram_tensor(name, shape, F32, kind="ExternalInput").ap()

    x = din("x", [T, D])
    w_in = din("w_in", [D, INW])
    sink = din("sink", [1, NH])
    gmlp_ln_g = din("gmlp_ln_g", [1, 512])
    gmlp_ln_b = din("gmlp_ln_b", [1, 512])
    w_spatial = din("w_spatial", [8, 128, 128])
    b_spatial = din("b_spatial", [8, 128])
    attn_norm_g = din("attn_norm_g", [1, 512])
    gmlp_norm_g = din("gmlp_norm_g", [1, 512])
    w_out = din("w_out", [D, D])
    ln1_g = din("ln1_g", [1, D])
    ln1_b = din("ln1_b", [1, D])
    w_ff1 = din("w_ff1", [D, DFF])
    w_ff2 = din("w_ff2", [DFF, D])
    ln2_g = din("ln2_g", [1, D])
    ln2_b = din("ln2_b", [1, D])
    out = nc.dram_tensor("out", [T, D], F32, kind="ExternalOutput").ap()
    if dbg:
        x1s = nc.dram_tensor("x1s", [T, D], F32, kind="ExternalOutput").ap()
    else:
        x1s = nc.dram_tensor("x1s", [T, D], F32).ap()

    R = Rec()
    es = ExitStack()
    AW = 53200
    big = es.enter_context(nc.sbuf_tensor("arena", [128, AW], F32))
    psum = es.enter_context(nc.psum_tensor("psum", [128, 4096], F32))
    A = Arena(big, AW)

    def bank(b):
        return psum[:, b * 512:(b + 1) * 512]

    def small(name, n):
        return Buf(A.alloc(F32, [n]), name)

    def smalls(name, n, k=2):
        return [small(f"{name}{i}", n) for i in range(k)]

    ident = Buf(A.alloc(BF16, [128]), "ident")
    epsA = small("epsA", 1)
    epsB = small("epsB", 1)
    persist_off = A.off

    W1OFF = AW - 16384
    A.off = W1OFF
    w_in_all = A.alloc(BF16, [8, INW])
    w_in_q = Buf(w_in_all[:, :, 0:512], "w_in_q")
    w_in_kv = Buf(w_in_all[:, :, 512:768], "w_in_kv")
    w_in_gu = Buf(w_in_all[:, :, 768:1280], "w_in_gu")
    w_in_gv = Buf(w_in_all[:, :, 1280:1792], "w_in_gv")
    w_in_parts = [w_in_q, w_in_kv, w_in_gu, w_in_gv]
    A.off = persist_off
    wq_perm = Buf(A.alloc(BF16, [8, 512]), "wq_perm")
    w_out_all = A.alloc(BF16, [8, D])
    w_out_bf = [Buf(w_out_all[:, c, :], f"w_out_bf{c}") for c in range(8)]
    wsT = Buf(A.alloc(BF16, [8, 128]), "wsT")
    biasT2 = Buf(A.alloc(BF16, [4, 768]), "biasT2")
    LG = Buf(A.alloc(F32, [512]), "LG")
    LB = Buf(A.alloc(F32, [512]), "LB")
    G1 = Buf(A.alloc(F32, [D]), "G1")
    B1 = Buf(A.alloc(F32, [D]), "B1")
    esink = small("esink", NH)
    gcol = small("gcol", 8)
    bsT = small("bsT", 8)

    xb = [Buf(A.alloc(BF16, [D]), f"xb{i}") for i in range(2)]
    xres = [Buf(A.alloc(F32, [D]), f"xres{i}") for i in range(2)]
    xT_all = A.alloc(BF16, [8, 512])
    xTq = [Buf(xT_all[:, :, b * 128:(b + 1) * 128], f"xTq{b}") for b in range(4)]
    qz = [Buf(A.alloc(BF16, [4, 4, 256]), f"qz{i}") for i in range(2)]
    kT = [Buf(A.alloc(BF16, [512]), f"kT{i}") for i in range(2)]
    vaug_all = A.alloc(BF16, [8, 2, 66])
    vaug = [Buf(vaug_all[:, s_, :, 0:65], f"vaug{s_}") for s_ in range(8)]
    gu_sb = [Buf(A.alloc(F32, [512]), f"gu_sb{i}") for i in range(4)]
    gv_sb = [Buf(A.alloc(F32, [512]), f"gv_sb{i}") for i in range(4)]
    vn = [Buf(A.alloc(BF16, [512]), f"vn{i}") for i in range(2)]
    sgu_f = [Buf(A.alloc(F32, [512]), f"sgu_f{i}") for i in range(2)]
    junk1 = Buf(A.alloc(BF16, [512]), "junk")
    junk = [junk1, junk1]
    NSG = 6
    sgu_n = [Buf(A.alloc(BF16, [512]), f"sgu_n{i}") for i in range(NSG)]
    sS2 = [Buf(A.alloc(F32, [768]), f"sS2_{i}") for i in range(2)]
    pT2 = [Buf(A.alloc(BF16, [768]), f"pT2_{i}") for i in range(2)]
    attn_f = [Buf(A.alloc(F32, [512]), f"attn_f{i}") for i in range(2)]
    attn_n = [Buf(A.alloc(BF16, [512]), f"attn_n{i}") for i in range(2)]
    mixT = Buf(A.alloc(BF16, [8, 128]), "mixT")
    ybuf = [Buf(A.alloc(F32, [D]), f"y{i}") for i in range(3)]
    tmpA = Buf(ybuf[2].ap[:, 0:384], "tmpA", extra=[ybuf[2].lt])
    tmpB = Buf(ybuf[2].ap[:, 384:768], "tmpB", extra=[ybuf[2].lt])
    ws_bf = Buf(sS2[1].ap[:, 0:512].bitcast(BF16).rearrange("p (g s) -> p g s", g=8), "ws_bf", extra=[sS2[1].lt])
    st6 = smalls("st6", 6)
    mv = smalls("mv", 2)
    lnv = smalls("lnv", 1)
    rstd = smalls("rstd", 1)
    nmr = smalls("nmr", 1)
    ssq_s = smalls("ssq_s", 1)
    lnv_s = smalls("lnv_s", 1)
    rs_s = smalls("rs_s", 1)
    ssq_a = smalls("ssq_a", 1)
    lnv_a = smalls("lnv_a", 1)
    rs_a = smalls("rs_a", 1)
    den = smalls("den", 4, 4)
    rinv = smalls("rinv", 4, 4)
    st12 = smalls("st12", 12)
    mv1 = smalls("mv1", 2)
    lnv1 = smalls("lnv1", 1)
    rstd1 = smalls("rstd1", 1)
    nmr1 = smalls("nmr1", 1)
    phaseA_end = A.off
    assert phaseA_end <= W1OFF, (phaseA_end, W1OFF)
    w1q = []
    for q in range(4):
        ap = big[:, W1OFF + q * 4096:W1OFF + (q + 1) * 4096].bitcast(BF16).rearrange("p (c n) -> p c n", c=8)
        extra = [b_.lt for b_ in w_in_parts] if q < 2 else []
        w1q.append(Buf(ap, f"w1q{q}", extra=extra))

    def load_w1q(q):
        lq = (q + 2) % 4
        R.dma("pool", lambda e: e.dma_start(out=w1q[q].ap, in_=w_ff1[:, lq * 1024:(lq + 1) * 1024].rearrange(
            "(c p) n -> p c n", p=128)), [], [w1q[q]], f"w1q{q}", 4194304)

    psT_ap = bank(0).bitcast(BF16).rearrange("p (c t) -> p c t", c=8)
    BL = [LT(f"bank{b}") for b in range(8)]
    psT = Buf(psT_ap, "psT", extra=[BL[0]])
    psM = Buf(bank(0), "psM", extra=[BL[0]])
    psQ = [Buf(bank(1), "psQ0", extra=[BL[1]]), Buf(bank(2), "psQ1", extra=[BL[2]])]
    psP = [psQ[0], psQ[0]]
    psOlo = Buf(bank(2)[:, 0:260], "psOlo", extra=[BL[2]])
    psOhi = Buf(bank(3)[:, 0:260], "psOhi", extra=[BL[3]])
    Sb = [Buf(psum[:, 2048:2816], "S0", extra=[BL[4], BL[5]]), Buf(psum[:, 3072:3840], "S1", extra=[BL[6], BL[7]])]
    psG = [[Buf(bank(3), "psG0", extra=[BL[3]]), Buf(bank(4), "psG1", extra=[BL[4]])],
           [Buf(bank(5), "psG2", extra=[BL[5]]), Buf(bank(6), "psG3", extra=[BL[6]])]]
    psV = Buf(bank(7)[:, 384:512], "psV", extra=[BL[7]])

    slopes = [2.0 ** (-(h + 1)) for h in range(NH)]

    R.dve(lambda e: e.memset(epsA.ap, EPS), [], [epsA], 1)
    R.dve(lambda e: e.memset(epsB.ap, EPS / (ALPHA * ALPHA)), [], [epsB], 1)
    R.dve(lambda e: e.memset(vaug_all[:, :, :, 64:66], 1.0), [], vaug, 32)

    idf = tmpA.ap[:, 0:128]
    idn = tmpB.ap[:, 0:128]
    R.pool(lambda e: e.iota(idf, [[1, 128]], base=0, channel_multiplier=-1, allow_small_or_imprecise_dtypes=True),
           [], [tmpA], 128)
    R.dve(lambda e: e.tensor_scalar(out=idn, in0=idf, scalar1=-1.0, scalar2=None, op0=ALU.mult), [tmpA], [tmpB], 128)
    R.dve(lambda e: e.tensor_tensor(out=idf, in0=idf, in1=idn, op=ALU.max), [tmpA, tmpB], [tmpA], 128)
    R.dve(lambda e: e.tensor_scalar(out=idf, in0=idf, scalar1=1.0, scalar2=None, op0=ALU.min), [tmpA], [tmpA], 128)
    R.dve(lambda e: e.tensor_scalar(out=ident.ap, in0=idf, scalar1=-1.0, scalar2=1.0, op0=ALU.mult, op1=ALU.add),
          [tmpA], [ident], 128)

    dd = tmpA.ap
    R.pool(lambda e: e.iota(dd.rearrange("p (j q) -> p j q", j=3), [[128, 3], [-1, 128]], base=-128,
                            channel_multiplier=1, allow_small_or_imprecise_dtypes=True), [tmpA], [tmpA], 384)
    R.dve(lambda e: e.tensor_scalar(out=tmpB.ap, in0=dd, scalar1=-1.0, scalar2=None, op0=ALU.mult), [tmpA], [tmpB], 384)
    R.dve(lambda e: e.tensor_tensor(out=dd, in0=dd, in1=tmpB.ap, op=ALU.max), [tmpA, tmpB], [tmpA], 384)
    R.dve(lambda e: e.tensor_scalar(out=tmpB.ap, in0=dd, scalar1=-128.0, scalar2=0.0, op0=ALU.add, op1=ALU.max),
          [tmpA], [tmpB], 384)
    R.dve(lambda e: e.tensor_scalar(out=tmpB.ap, in0=tmpB.ap, scalar1=1.0, scalar2=NEG, op0=ALU.min, op1=ALU.mult),
          [tmpB], [tmpB], 384)
    b2v = biasT2.ap.rearrange("p i (j h q) -> p i j h q", j=3, h=2)
    dd3 = dd.rearrange("p (j q) -> p j q", j=3)
    mk3 = tmpB.ap.rearrange("p (j q) -> p j q", j=3)
    for ic in range(4):
        for h2 in range(2):
            R.dve(lambda e, ic=ic, h2=h2: e.scalar_tensor_tensor(out=b2v[:, ic, :, h2, :], in0=dd3, scalar=-slopes[ic + 4 * h2],
                                                                 in1=mk3, op0=ALU.mult, op1=ALU.add),
                  [tmpA, tmpB], [biasT2], 384)
    for i in range(2):
        R.pool(lambda e, i=i: e.iota(qz[i].ap.rearrange("p a b c -> p (a b c)"), [[0, 4096]], base=0, channel_multiplier=0,
                                     allow_small_or_imprecise_dtypes=True), [], [qz[i]], 2048)

    R.dma("pool", lambda e: e.dma_start(out=w_in_all, in_=w_in.rearrange("(c p) n -> p c n", p=128)),
          [], w_in_parts, "w_in", 128 * 8 * INW * 4)
    for ic in range(4):
        for h2 in range(2):
            hh = ic + 4 * h2
            R.dve(lambda e, ic=ic, h2=h2, hh=hh: e.tensor_copy(
                out=wq_perm.ap[:, :, ic * 128 + h2 * 64:ic * 128 + h2 * 64 + 64],
                in_=w_in_all[:, :, hh * 64:hh * 64 + 64]), [w_in_q], [wq_perm], 512)

    SETUP_LAT = [45000.0]

    def small_dma(key, dst, src, nb=4096, slow=False):
        if slow:
            R.dma("sp", lambda e: e.dma_start(out=dst_ap(dst), in_=src, allow_slow_non_contiguous=True), [], [dst[0]], key, nb,
                  lat=SETUP_LAT[0])
        else:
            R.dma("sp", lambda e: e.dma_start(out=dst_ap(dst), in_=src), [], [dst[0]], key, nb, lat=SETUP_LAT[0])

    def dst_ap(d):
        return d[1]

    small_dma("m_gc0", (gcol, gcol.ap[:, 0:4]), attn_norm_g.rearrange("o (c p) -> p (o c)", p=128), slow=True)
    small_dma("m_gc1", (gcol, gcol.ap[:, 4:8]), gmlp_norm_g.rearrange("o (c p) -> p (o c)", p=128), slow=True)
    small_dma("m_bs", (bsT, bsT.ap), b_spatial.rearrange("g t -> t g"), slow=True)
    small_dma("m_lg", (LG, LG.ap), gmlp_ln_g.partition_broadcast(128), 262144)
    small_dma("m_lb", (LB, LB.ap), gmlp_ln_b.partition_broadcast(128), 262144)
    small_dma("m_g1", (G1, G1.ap), ln1_g.partition_broadcast(128), 524288)
    small_dma("m_b1", (B1, B1.ap), ln1_b.partition_broadcast(128), 524288)
    small_dma("m_sk", (esink, esink.ap), sink.partition_broadcast(128))
    R.act(lambda e: e.activation(out=esink.ap, in_=esink.ap, func=AF.Exp), [esink], [esink], 8, "el")
    for c in range(8):
        st = ybuf[c % 2]
        R.dma("sp", lambda e, c=c, st=st: e.dma_start(out=st.ap, in_=w_out[c * 128:(c + 1) * 128, :]),
              [], [st], f"stage{c % 2}", 524288, lat=(55000.0 if c < 2 else 5000.0))
        R.dve(lambda e, c=c, st=st: e.tensor_scalar(out=w_out_bf[c].ap, in0=st.ap, scalar1=gcol.ap[:, c:c + 1],
                                                     scalar2=None, op0=ALU.mult), [st, gcol], [w_out_bf[c]], 1024)
    wsf = ybuf[2]
    rowsum = small("rowsum", 8)
    R.dma("sp", lambda e: e.dma_start(out=wsf.ap.rearrange("p (g s) -> p g s", g=8), in_=w_spatial.rearrange("g t s -> t g s")),
          [], [wsf], "m_wsf", 524288, lat=60000.0)
    def fold_lb():
        R.dve(lambda e: e.tensor_reduce(out=rowsum.ap, in_=wsf.ap.rearrange("p (g s) -> p g s", g=8), axis=mybir.AxisListType.X,
                                        op=ALU.add), [wsf, wq_perm, qz[0]], [rowsum], 1024)
        for g in range(8):
            R.dve(lambda e, g=g: e.tensor_scalar(out=LB.ap[:, g * 64:(g + 1) * 64], in0=LB.ap[:, g * 64:(g + 1) * 64],
                                                 scalar1=rowsum.ap[:, g:g + 1], scalar2=bsT.ap[:, g:g + 1],
                                                 op0=ALU.mult, op1=ALU.add), [LB, rowsum, bsT], [LB], 64)

    R.dma("pool", lambda e: e.dma_start(out=ws_bf.ap, in_=w_spatial.rearrange("g t s -> t g s")),
          [], [ws_bf], "ws", 524288)
    R.pe(lambda e: [e.transpose(psT.ap[:, g, :], ws_bf.ap[:, g, :], ident.ap) for g in range(8)],
         [ws_bf, ident], [psT], [128] * 8)
    R.dve(lambda e: e.tensor_copy(out=wsT.ap, in_=psT.ap), [psT], [wsT], 1024)

    def rsqrt_act(src_ap, src_buf, eps_buf, scale, lnb, outb):
        R.act(lambda e: e.activation(out=lnb.ap, in_=src_ap, func=AF.Ln, bias=eps_buf.ap, scale=scale),
              [src_buf, eps_buf], [lnb], 1, "el")
        R.act(lambda e: e.activation(out=outb.ap, in_=lnb.ap, func=AF.Exp, scale=-0.5), [lnb], [outb], 1, "el")

    def front(u):
        for b in range(4):
            blk = 4 * u + b
            xbb = xb[blk % 2]
            R.dma("pool", lambda e, blk=blk, xbb=xbb: e.dma_start(out=xbb.ap, in_=x[blk * 128:(blk + 1) * 128, :]),
                  [], [xbb], f"xb{blk % 2}", 524288)
            R.pe(lambda e, xbb=xbb: [e.transpose(psT.ap[:, c, :], xbb.ap[:, c * 128:(c + 1) * 128], ident.ap)
                                     for c in range(8)], [xbb, ident], [psT], [128] * 8)
            R.dve(lambda e, b=b: e.tensor_copy(out=xTq[b].ap, in_=psT.ap), [psT], [xTq[b]], 1024)
        for grp in range(5):
            pq = psQ[grp % 2]
            if grp < 4:
                wsrc, rd, c0 = wq_perm.ap, [wq_perm], grp * 128
            else:
                wsrc, rd, c0 = w_in_all, [w_in_kv], 512
            R.pe(lambda e, wsrc=wsrc, c0=c0, pq=pq: [
                e.matmul(pq.ap, lhsT=wsrc[:, c, c0:c0 + 128], rhs=xT_all[:, c, :], start=(c == 0), stop=(c == 7))
                for c in range(8)], rd + xTq, [pq], [512] * 8)
            if grp < 4:
                dst = qz[u % 2]
                R.dve(lambda e, grp=grp, pq=pq, dst=dst: e.tensor_copy(
                    out=dst.ap[0:64, grp, :, 0:128], in_=pq.ap[0:64, :].rearrange("p (b q) -> p b q", b=4)), [pq], [dst], 512)
                R.act(lambda e, grp=grp, pq=pq, dst=dst: e.activation(
                    out=dst.ap[64:128, grp, :, 128:256], in_=pq.ap[64:128, :].rearrange("p (b q) -> p b q", b=4),
                    func=AF.Copy), [pq], [dst], 512)
            else:
                dst = kT[u % 2]
                R.act(lambda e, pq=pq, dst=dst: e.activation(out=dst.ap, in_=pq.ap, func=AF.Copy), [pq], [dst], 512)
        gel = []
        for b in range(4):
            blk = 4 * u + b
            xs = xTq[b]
            R.pe(lambda e, xs=xs: [e.matmul(psV.ap, lhsT=xs.ap[:, c, :], rhs=w_in_all[:, c, 640:768],
                                            start=(c == 0), stop=(c == 7)) for c in range(8)],
                 [xs, w_in_kv], [psV], [128] * 8)
            va = vaug[blk % 8]
            R.dve(lambda e, va=va: e.tensor_copy(out=va.ap[:, :, 0:64], in_=psV.ap.rearrange("p (g d) -> p g d", g=2)),
                  [psV], [va], 128)
            pgu, pgv = psG[b % 2]
            R.pe(lambda e, xs=xs, pgu=pgu: [e.matmul(pgu.ap, lhsT=xs.ap[:, c, :], rhs=w_in_all[:, c, 768:1280],
                                                     start=(c == 0), stop=(c == 7)) for c in range(8)],
                 [xs, w_in_gu], [pgu], [512] * 8)
            gel.append((gu_sb[b], pgu))
            R.pe(lambda e, xs=xs, pgv=pgv: [e.matmul(pgv.ap, lhsT=xs.ap[:, c, :], rhs=w_in_all[:, c, 1280:1792],
                                                     start=(c == 0), stop=(c == 7)) for c in range(8)],
                 [xs, w_in_gv], [pgv], [512] * 8)
            gel.append((gv_sb[b], pgv))
            if b % 2 == 1:
                srcs = [g_[1] for g_ in gel]
                for (dst_, src_) in gel:
                    R.act(lambda e, dst_=dst_, src_=src_: e.activation(out=dst_.ap, in_=src_.ap, func=AF.Gelu),
                          srcs, [dst_], 512, "gelu")
                gel = []

    def sgu_vec(b, blk):
        p = blk % 2
        g = gv_sb[b]
        R.dve(lambda e: e.bn_stats(out=st6[p].ap, in_=g.ap), [g], [st6[p]], 512)
        R.dve(lambda e: e.bn_aggr(out=mv[p].ap, in_=st6[p].ap), [st6[p]], [mv[p]], 26)
        rsqrt_act(mv[p].ap[:, 1:2], mv[p], epsA, 1.0, lnv[p], rstd[p])
        R.dve(lambda e: e.scalar_tensor_tensor(out=nmr[p].ap, in0=mv[p].ap[:, 0:1], scalar=-1.0, in1=rstd[p].ap,
                                               op0=ALU.mult, op1=ALU.mult), [mv[p], rstd[p]], [nmr[p]], 1)
        R.act(lambda e: e.activation(out=vn[p].ap, in_=g.ap, func=AF.Identity, bias=nmr[p].ap, scale=rstd[p].ap),
              [g, nmr[p], rstd[p]], [vn[p]], 512)

    def sgu_mm(b, blk):
        p = blk % 2
        R.pe(lambda e: [e.matmul(psM.ap[:, g * 64:(g + 1) * 64], lhsT=wsT.ap[:, g, :], rhs=vn[p].ap[:, g * 64:(g + 1) * 64],
                                 start=True, stop=True) for g in range(8)], [wsT, vn[p]], [psM], [128] * 8)
        R.dve(lambda e: e.tensor_tensor(out=sgu_f[p].ap, in0=psM.ap, in1=LG.ap, op=ALU.mult), [psM, LG], [sgu_f[p]], 512)
        R.dve(lambda e: e.tensor_tensor(out=sgu_f[p].ap, in0=sgu_f[p].ap, in1=LB.ap, op=ALU.add), [sgu_f[p], LB], [sgu_f[p]], 512)
        R.dve(lambda e: e.tensor_tensor(out=sgu_f[p].ap, in0=sgu_f[p].ap, in1=gu_sb[b].ap, op=ALU.mult),
              [sgu_f[p], gu_sb[b]], [sgu_f[p]], 512)
        R.act(lambda e: e.activation(out=junk[0].ap, in_=sgu_f[p].ap, func=AF.Square, accum_out=ssq_s[p].ap),
              [sgu_f[p]], [junk[0], ssq_s[p]], 512)
        rsqrt_act(ssq_s[p].ap, ssq_s[p], epsA, 1.0 / 512.0, lnv_s[p], rs_s[p])
        dst = sgu_n[blk % NSG]
        R.act(lambda e: e.activation(out=dst.ap, in_=sgu_f[p].ap, func=AF.Identity, scale=rs_s[p].ap),
              [sgu_f[p], rs_s[p]], [dst], 512)

    def attention(m):
        p = m % 2
        js = [j for j in range(3) if 0 <= m - 1 + j < NB]
        lo, hi = js[0] * 256, (js[-1] + 1) * 256
        qs = qz[(m // 4) % 2]
        qb = m % 4
        af = attn_f[p]
        for ic in range(4):
            pS = Sb[ic % 2]
            sSb = sS2[ic % 2]
            pTb = pT2[ic % 2]
            kks = [(j, kT[((m - 1 + j) // 4) % 2], ((m - 1 + j) % 4) * 128) for j in js]
            R.pe(lambda e, kks=kks, ic=ic, pS=pS: [
                e.matmul(pS.ap[:, j * 256:(j + 1) * 256], lhsT=ks.ap[:, koff:koff + 128],
                         rhs=qs.ap[:, ic, qb, :], start=True, stop=True)
                for (j, ks, koff) in kks], [k[1] for k in kks] + [qs], [pS], [256] * len(js))
            R.dve(lambda e, ic=ic, pS=pS, sSb=sSb: e.scalar_tensor_tensor(
                out=sSb.ap[:, lo:hi], in0=pS.ap[:, lo:hi], scalar=0.125, in1=biasT2.ap[:, ic, lo:hi],
                op0=ALU.mult, op1=ALU.add), [pS, biasT2], [sSb], hi - lo)
            R.act(lambda e, sSb=sSb, pTb=pTb: e.activation(out=pTb.ap[:, lo:hi], in_=sSb.ap[:, lo:hi], func=AF.Exp),
                  [sSb], [pTb], hi - lo, "el")
            vas = [(j, vaug[(m - 1 + j) % 8]) for j in js]
            for h2 in range(2):
                pO = psOlo if h2 == 0 else psOhi
                R.pe(lambda e, vas=vas, h2=h2, ic=ic, pO=pO, pTb=pTb: [
                    e.matmul(pO.ap[:, ic * 65:ic * 65 + 65], lhsT=pTb.ap[:, j * 256 + h2 * 128:j * 256 + h2 * 128 + 128],
                             rhs=va.ap[:, h2, :], start=(j == js[0]), stop=(j == js[-1])) for (j, va) in vas],
                     [pTb] + [v[1] for v in vas], [pO], [128] * len(js))
        for half in range(2):
            pO = psOlo if half == 0 else psOhi
            o3 = pO.ap.rearrange("p (h d) -> p h d", h=4)
            dn, ri = den[2 * p + half], rinv[2 * p + half]
            R.dve(lambda e, o3=o3, dn=dn, half=half: e.tensor_tensor(
                out=dn.ap, in0=o3[:, :, 64], in1=esink.ap[:, half * 4:(half + 1) * 4], op=ALU.add),
                  [pO, esink], [dn], 4)
            R.dve(lambda e, dn=dn, ri=ri: e.reciprocal(out=ri.ap, in_=dn.ap), [dn], [ri], 4)
            R.dve(lambda e, o3=o3, ri=ri, half=half: e.tensor_tensor(
                out=af.ap[:, half * 256:(half + 1) * 256].rearrange("p (h d) -> p h d", h=4),
                in0=o3[:, :, 0:64], in1=ri.ap.unsqueeze(2).to_broadcast([128, 4, 64]), op=ALU.mult),
                  [pO, ri], [af], 256)
        R.act(lambda e: e.activation(out=junk[1].ap, in_=af.ap, func=AF.Square, accum_out=ssq_a[p].ap),
              [af], [junk[1], ssq_a[p]], 512)
        rsqrt_act(ssq_a[p].ap, ssq_a[p], epsA, 1.0 / 512.0, lnv_a[p], rs_a[p])
        R.act(lambda e: e.activation(out=attn_n[p].ap, in_=af.ap, func=AF.Identity, scale=rs_a[p].ap),
              [af, rs_a[p]], [attn_n[p]], 512)

    def mix(m):
        p = m % 2
        p3 = m % 3
        xr = xres[p]
        R.dma("sp", lambda e: e.dma_start(out=xr.ap, in_=x[m * 128:(m + 1) * 128, :]), [], [xr], f"xres{p}", 524288)
        sg = sgu_n[m % NSG]
        an = attn_n[p]
        R.pe(lambda e: [e.transpose(psT.ap[:, c, :], (an.ap[:, c * 128:(c + 1) * 128] if c < 4 else
                                                      sg.ap[:, (c - 4) * 128:(c - 3) * 128]), ident.ap) for c in range(8)],
             [an, sg, ident], [psT], [128] * 8)
        R.dve(lambda e: e.tensor_copy(out=mixT.ap, in_=psT.ap), [psT], [mixT], 1024)
        yb = ybuf[p3]
        for hf in range(2):
            pp = psP[hf]
            R.pe(lambda e, hf=hf, pp=pp: [e.matmul(pp.ap, lhsT=mixT.ap[:, c, :], rhs=w_out_bf[c].ap[:, hf * 512:(hf + 1) * 512],
                                                   start=(c == 0), stop=(c == 7)) for c in range(8)],
                 [mixT] + w_out_bf, [pp], [512] * 8)
            R.dve(lambda e, hf=hf, pp=pp: e.scalar_tensor_tensor(
                out=yb.ap[:, hf * 512:(hf + 1) * 512], in0=pp.ap, scalar=1.0 / ALPHA,
                in1=xr.ap[:, hf * 512:(hf + 1) * 512], op0=ALU.mult, op1=ALU.add), [pp, xr], [yb], 512)
        for hf in range(2):
            R.dve(lambda e, hf=hf: e.bn_stats(out=st12[p].ap[:, hf * 6:(hf + 1) * 6], in_=yb.ap[:, hf * 512:(hf + 1) * 512]),
                  [yb], [st12[p]], 512)
        R.dve(lambda e: e.bn_aggr(out=mv1[p].ap, in_=st12[p].ap), [st12[p]], [mv1[p]], 52)
        rsqrt_act(mv1[p].ap[:, 1:2], mv1[p], epsB, 1.0, lnv1[p], rstd1[p])
        R.dve(lambda e: e.scalar_tensor_tensor(out=nmr1[p].ap, in0=mv1[p].ap[:, 0:1], scalar=-1.0, in1=rstd1[p].ap,
                                               op0=ALU.mult, op1=ALU.mult), [mv1[p], rstd1[p]], [nmr1[p]], 1)
        R.act(lambda e: e.activation(out=yb.ap, in_=yb.ap, func=AF.Identity, bias=nmr1[p].ap, scale=rstd1[p].ap),
              [yb, nmr1[p], rstd1[p]], [yb], 1024)
        if m == NB - 1:
            for (cst, aop) in ((G1, ALU.mult), (B1, ALU.add)):
                R.dve(lambda e, cst=cst, aop=aop: e.tensor_tensor(out=yb.ap, in0=yb.ap, in1=cst.ap, op=aop), [yb, cst], [yb], 1024)
        for (cst, aop) in (((G1, ALU.mult), (B1, ALU.add)) if m < NB - 1 else ()):
            for qq in range(4):
                R.pool(lambda e, cst=cst, aop=aop, qq=qq: e.tensor_tensor(
                    out=yb.ap[:, qq * 256:(qq + 1) * 256], in0=yb.ap[:, qq * 256:(qq + 1) * 256],
                    in1=cst.ap[:, qq * 256:(qq + 1) * 256], op=aop), [yb, cst], [yb], 256)
        R.dma("sp", lambda e: e.dma_start(out=x1s[m * 128:(m + 1) * 128, :], in_=yb.ap), [yb], [], f"x1t{p3}", 524288)

    for u in range(NU + 1):
        if u < NU:
            R.cur_boost = 1e8 if u == 0 else 0.0
            front(u)
            R.cur_boost = 0.0
        if u == 0:
            fold_lb()
        if u == max(NU - 2, 0):
            load_w1q(2)
            load_w1q(3)
        if u == NU - 1:
            load_w1q(0)
            load_w1q(1)
        for i in range(4):
            blk = 4 * u + i
            m = 4 * u - 1 + i
            do_sgu = u < NU
            do_att = (0 <= m < NB) and (u < NU or i == 0)
            if do_sgu:
                sgu_vec(i, blk)
            if do_att:
                attention(m)
            if do_sgu:
                sgu_mm(i, blk)
            if do_att:
                mix(m)

    R.barrier()
    A.off = persist_off
    w2_all = A.alloc(BF16, [32, D])
    w2q = [Buf(w2_all[:, q * 8:(q + 1) * 8, :], f"w2q{q}") for q in range(4)]
    G2 = Buf(A.alloc(F32, [D]), "G2")
    B2 = Buf(A.alloc(F32, [D]), "B2")
    hT_all = A.alloc(BF16, [32, 256])
    hT = [Buf(hT_all[:, fc, :], f"hT{fc}") for fc in range(32)]
    x1 = [Buf(A.alloc(F32, [2, D]), f"x1_{i}") for i in range(2)]
    x1b = [Buf(A.alloc(BF16, [2, D]), f"x1b{i}") for i in range(2)]
    x1T_all = [A.alloc(BF16, [8, 256]) for i in range(2)]
    x1T = [[Buf(x1T_all[i][:, :, s_ * 128:(s_ + 1) * 128], f"x1T{i}_{s_}") for s_ in range(2)] for i in range(2)]
    rbuf = [Buf(A.alloc(F32, [256]), f"rbuf{i}") for i in range(3)]
    y2 = [Buf(A.alloc(F32, [2, D]), f"y2_{i}") for i in range(2)]
    stB = smalls("stB", 12)
    mvB = smalls("mvB", 2)
    lnvB = smalls("lnvB", 1)
    rstdB = smalls("rstdB", 1)
    nmrB = smalls("nmrB", 1)
    phaseB_end = A.off
    assert phaseB_end <= W1OFF, (phaseB_end, W1OFF)
    if verbose:
        print("arena words: phaseA", phaseA_end, "phaseB", phaseB_end, "of", AW)

    psTB = Buf(psT_ap, "psTB", extra=[BL[0]])
    psH = [Buf(bank(1 + i)[:, 0:256], f"psH{i}", extra=[BL[1 + i]]) for i in range(3)]
    psY = [[Buf(bank(4 + 2 * s_ + hf), f"psY{s_}{hf}", extra=[BL[4 + 2 * s_ + hf]]) for hf in range(2)] for s_ in range(2)]

    def load_w2():
        for q in range(4):
            R.dma("pool", lambda e, q=q: e.dma_start(out=w2q[q].ap, in_=w_ff2[q * 1024:(q + 1) * 1024, :].rearrange(
                "(c p) n -> p c n", p=128)), [x1b[0]], [w2q[q]], f"w2q{q}", 4194304)
    SETUP_LAT[0] = None
    small_dma("m_g2", (G2, G2.ap), ln2_g.partition_broadcast(128), 524288)
    small_dma("m_b2", (B2, B2.ap), ln2_b.partition_broadcast(128), 524288)

    def b_load(i):
        xs = x1[i % 2]
        xbb = x1b[i % 2]
        src = x1s[i * 256:(i + 1) * 256, :].rearrange("(s p) d -> p s d", p=128)
        R.dma("sp", lambda e: e.dma_start(out=xs.ap, in_=src), [], [xs], f"x1L{i % 2}", 1048576)
        R.dma("pool", lambda e: e.dma_start(out=xbb.ap, in_=src), [], [xbb], f"x1bL{i % 2}", 1048576)

    def b_transpose(i):
        xbb = x1b[i % 2]
        for s_ in range(2):
            R.pe(lambda e, s_=s_: [e.transpose(psTB.ap[:, c, :], xbb.ap[:, s_, c * 128:(c + 1) * 128], ident.ap)
                                   for c in range(8)], [xbb, ident], [psTB], [128] * 8)
            dst = x1T[i % 2][s_]
            R.dve(lambda e, dst=dst: e.tensor_copy(out=dst.ap, in_=psTB.ap), [psTB], [dst], 1024)

    def b_ff1(i):
        xt = x1T_all[i % 2]
        for fc in range(32):
            ph = psH[fc % 3]
            rb = rbuf[fc % 3]
            R.pe(lambda e, fc=fc, ph=ph: [e.matmul(ph.ap, lhsT=w1q[(fc // 8 + 2) % 4].ap[:, c, (fc % 8) * 128:(fc % 8 + 1) * 128], rhs=xt[:, c, :],
                                                   start=(c == 0), stop=(c == 7)) for c in range(8)],
                 [w1q[(fc // 8 + 2) % 4]] + x1T[i % 2], [ph], [256] * 8)
            R.act(lambda e, ph=ph, rb=rb: e.activation(out=rb.ap, in_=ph.ap, func=AF.Relu), [ph], [rb], 256)
            R.dve(lambda e, fc=fc, rb=rb: e.tensor_tensor(out=hT[fc].ap, in0=rb.ap, in1=rb.ap, op=ALU.mult),
                  [rb], [hT[fc]], 256)

    def b_ff2(i):
        xs = x1[i % 2]
        yo = y2[i % 2]
        for s_ in range(2):
            for q in range(4):
                R.pe(lambda e, s_=s_, q=q: [e.matmul(psY[s_][hf].ap, lhsT=hT[fc].ap[:, s_ * 128:(s_ + 1) * 128],
                                                     rhs=w2_all[:, fc, hf * 512:(hf + 1) * 512],
                                                     start=(fc == 0), stop=(fc == 31))
                                            for fc in range(q * 8, q * 8 + 8) for hf in range(2)],
                     hT[q * 8:q * 8 + 8] + [w2q[q]], psY[s_], [512] * 16)
        for s_ in range(2):
            p = s_
            for hf in range(2):
                py = psY[s_][hf]
                R.dve(lambda e, s_=s_, hf=hf, py=py: e.scalar_tensor_tensor(
                    out=yo.ap[:, s_, hf * 512:(hf + 1) * 512], in0=py.ap, scalar=1.0 / ALPHA,
                    in1=xs.ap[:, s_, hf * 512:(hf + 1) * 512], op0=ALU.mult, op1=ALU.add), [py, xs], [yo], 512)
            for hf in range(2):
                R.dve(lambda e, s_=s_, hf=hf, p=p: e.bn_stats(out=stB[p].ap[:, hf * 6:(hf + 1) * 6],
                                                         in_=yo.ap[:, s_, hf * 512:(hf + 1) * 512]), [yo], [stB[p]], 512)
            R.dve(lambda e, p=p: e.bn_aggr(out=mvB[p].ap, in_=stB[p].ap), [stB[p]], [mvB[p]], 52)
            rsqrt_act(mvB[p].ap[:, 1:2], mvB[p], epsB, 1.0, lnvB[p], rstdB[p])
            R.dve(lambda e, p=p: e.scalar_tensor_tensor(out=nmrB[p].ap, in0=mvB[p].ap[:, 0:1], scalar=-1.0, in1=rstdB[p].ap,
                                                        op0=ALU.mult, op1=ALU.mult), [mvB[p], rstdB[p]], [nmrB[p]], 1)
            R.act(lambda e, s_=s_, p=p: e.activation(out=yo.ap[:, s_, :], in_=yo.ap[:, s_, :], func=AF.Identity,
                                                     bias=nmrB[p].ap, scale=rstdB[p].ap), [yo, nmrB[p], rstdB[p]], [yo], 1024)
            aff = R.dve if i == NI - 1 else R.pool
            aff(lambda e, s_=s_: e.tensor_tensor(out=yo.ap[:, s_, :], in0=yo.ap[:, s_, :], in1=G2.ap, op=ALU.mult),
                [yo, G2], [yo], 1024)
            aff(lambda e, s_=s_: e.tensor_tensor(out=yo.ap[:, s_, :], in0=yo.ap[:, s_, :], in1=B2.ap, op=ALU.add),
                [yo, B2], [yo], 1024)
        R.dma("sp", lambda e: e.dma_start(out=out[i * 256:(i + 1) * 256, :].rearrange("(s p) d -> p s d", p=128), in_=yo.ap),
              [yo], [], f"out{i % 2}", 1048576)

    b_load(0)
    load_w2()
    b_transpose(0)
    for i in range(NI):
        if i + 1 < NI:
            b_load(i + 1)
        b_ff1(i)
        if i + 1 < NI:
            b_transpose(i + 1)
        b_ff2(i)

    R.emit(nc, es)
    if verbose:
        print("sim times us", [t / 1e3 for t in R.sim_times], "busy", R.sim_busy, "ops", len(R.all))
    es.close()
    return nc


_WNAMES = ["w_in", "sink", "gmlp_ln_g", "gmlp_ln_b", "w_spatial", "b_spatial", "attn_norm_g", "gmlp_norm_g",
           "w_out", "ln1_g", "ln1_b", "w_ff1", "w_ff2", "ln2_g", "ln2_b"]


def kernel(**inputs):
    x = np.ascontiguousarray(np.asarray(inputs["x"], dtype=np.float32))
    B, S, _ = x.shape
    shared = {}
    for n in _WNAMES:
        a = np.asarray(inputs[n], dtype=np.float32)
        a = a[0]
        if a.ndim == 1:
            a = a[None, :]
        shared[n] = np.ascontiguousarray(a)
    nc = build_nc(S)
    in_maps = []
    for b in range(B):
        m = {"x": x[b]}
        m.update(shared)
        in_maps.append(m)
    res = run_bass_kernel_spmd(nc, in_maps, core_ids=list(range(B)))
    return np.stack([np.asarray(r["out"], dtype=np.float32) for r in res.results], axis=0)
```
